# Optimizing a Trainium2 kernel written in Bass

```python
import jax, jax.numpy as jnp
from jax import lax
import numpy as np

D_MODEL = 4096
BATCH = 4
SEQ = 4096
DEPTH = 2
DEC_BATCH = 2
DEC_SEQ = 4096
PAST_LEN = 128

MEM_TOKENS = 256
MIX_WIDTH = D_MODEL
GLA_WIDTH = MIX_WIDTH // 2
MLSTM_WIDTH = MIX_WIDTH - GLA_WIDTH
GLA_HEADS = 4
GLA_DV = GLA_WIDTH // GLA_HEADS
GLA_DK = GLA_DV // 2
GLA_QK = GLA_HEADS * GLA_DK
GLA_RANK = 16
GLA_TAU = 16.0
GLA_MIN_LOG_DECAY = -1.0
MLSTM_HEADS = 4
MLSTM_DV = MLSTM_WIDTH // MLSTM_HEADS
MLSTM_DK = MLSTM_DV // 2
MLSTM_QK = MLSTM_HEADS * MLSTM_DK
CHUNK = 64
XATTN_HEADS = 4
XATTN_HEAD_DIM = 128
XATTN_WIDTH = XATTN_HEADS * XATTN_HEAD_DIM
D_FF = 11008
EPS = 1e-6
NEG_BIG = -1e30

IN_SIZES = (GLA_QK, GLA_QK, GLA_WIDTH, GLA_WIDTH, GLA_RANK, GLA_RANK,
            MLSTM_QK, MLSTM_QK, MLSTM_WIDTH, MLSTM_WIDTH, 4 * MLSTM_HEADS)
IN_WIDTH = sum(IN_SIZES)
IN_SPLITS = tuple(int(s) for s in np.cumsum(IN_SIZES)[:-1])

kernel_name = "hymba_gla_mlstm_macaron_encoder"


def rmsnorm(x, w):
    xf = x.astype(jnp.float32)
    y = xf * lax.rsqrt(jnp.mean(xf * xf, axis=-1, keepdims=True) + EPS)
    return (y * w.astype(jnp.float32)).astype(x.dtype)


def head_rmsnorm(x, w):
    H, d = x.shape[-2:]
    y = x * lax.rsqrt(jnp.mean(x * x, axis=-1, keepdims=True) + EPS)
    return y * w.reshape(H, d).astype(jnp.float32)


def swiglu(x, w_gate, w_up, w_down):
    return (jax.nn.silu(x @ w_gate) * (x @ w_up)) @ w_down


def to_chunks(t):
    B, L, H, d = t.shape
    return t.reshape(B, L // CHUNK, CHUNK, H, d).transpose(1, 0, 3, 2, 4)


def from_chunks(t):
    nc, B, H, C, d = t.shape
    return t.transpose(1, 0, 3, 2, 4).reshape(B, nc * C, H, d)


def gate_chunks(t):
    B, L, H = t.shape
    return t.reshape(B, L // CHUNK, CHUNK, H).transpose(1, 0, 3, 2)


def gla_scan(q, k, v, g):
    B, L, H, dk = q.shape
    dv = v.shape[-1]
    qc, kc, vc, gc = to_chunks(q), to_chunks(k), to_chunks(v), to_chunks(g)
    b = jnp.cumsum(gc, axis=3)
    b_last = b[:, :, :, -1:, :]
    q_dec = qc * jnp.exp(b)
    k_inv = kc * jnp.exp(-b)
    k_end = kc * jnp.exp(b_last - b)
    mask = jnp.tril(jnp.ones((CHUNK, CHUNK), dtype=bool))
    a = jnp.where(mask, jnp.einsum("nbhid,nbhjd->nbhij", q_dec, k_inv), 0.0)
    o_intra = jnp.einsum("nbhij,nbhjv->nbhiv", a, vc)

    def step(S, xs):
        qd, ke, vv, bl = xs
        o = jnp.einsum("bhid,bhdv->bhiv", qd, S)
        S = jnp.exp(bl[:, :, 0, :])[..., None] * S + jnp.einsum("bhid,bhiv->bhdv", ke, vv)
        return S, o

    S0 = jnp.zeros((B, H, dk, dv), jnp.float32)
    _, o_inter = lax.scan(step, S0, (q_dec, k_end, vc, b_last))
    return from_chunks(o_intra + o_inter)


def mlstm_scan(q, k, v, log_i, log_f):
    B, L, H, dk = q.shape
    dv = v.shape[-1]
    qc, kc, vc = to_chunks(q), to_chunks(k), to_chunks(v)
    ic, fc = gate_chunks(log_i), gate_chunks(log_f)
    b = jnp.cumsum(fc, axis=-1)
    g = b[..., -1]
    mask = jnp.tril(jnp.ones((CHUNK, CHUNK), dtype=bool))
    log_d = jnp.where(mask, b[..., :, None] - b[..., None, :] + ic[..., None, :], NEG_BIG)
    m_intra = jnp.max(log_d, axis=-1)
    s = jnp.einsum("nbhjd,nbhid->nbhji", qc, kc) * jnp.exp(log_d - m_intra[..., None])
    num_intra = jnp.einsum("nbhji,nbhiv->nbhjv", s, vc)
    den_intra = jnp.sum(s, axis=-1)
    a_end = g[..., None] - b + ic

    def step(carry, xs):
        Cs, ns, m = carry
        qq, kk, vv, bb, gg, ae, mi, nu, de = xs
        m_j = jnp.maximum(bb + m[..., None], mi)
        w_inter = jnp.exp(bb + m[..., None] - m_j)
        w_intra = jnp.exp(mi - m_j)
        num = w_inter[..., None] * jnp.einsum("bhjd,bhdv->bhjv", qq, Cs) + w_intra[..., None] * nu
        den = w_inter * jnp.einsum("bhjd,bhd->bhj", qq, ns) + w_intra * de
        h = num / jnp.maximum(jnp.abs(den), jnp.exp(-m_j))[..., None]
        m_new = jnp.maximum(gg + m, jnp.max(ae, axis=-1))
        scale_old = jnp.exp(gg + m - m_new)
        kw = kk * jnp.exp(ae - m_new[..., None])[..., None]
        Cs = scale_old[..., None, None] * Cs + jnp.einsum("bhid,bhiv->bhdv", kw, vv)
        ns = scale_old[..., None] * ns + jnp.sum(kw, axis=2)
        return (Cs, ns, m_new), h

    carry0 = (jnp.zeros((B, H, dk, dv), jnp.float32),
              jnp.zeros((B, H, dk), jnp.float32),
              jnp.full((B, H), NEG_BIG, jnp.float32))
    _, h = lax.scan(step, carry0, (qc, kc, vc, b, g, a_end, m_intra, num_intra, den_intra))
    return from_chunks(h)


def flip(t):
    return jnp.flip(t, axis=1)


def parallel_mixer(h, w_in, gla_w_lr, gla_b_lr, gla_out_norm, mlstm_gate_b, mlstm_out_norm, w_out):
    B, L, _ = h.shape
    proj = (h @ w_in).astype(jnp.float32)
    (gq, gk, gv, gg, glr_f, glr_b, mq, mk, mv, mo, mgates) = jnp.split(proj, IN_SPLITS, axis=-1)

    q = gq.reshape(B, L, GLA_HEADS, GLA_DK) * (GLA_DK ** -0.5)
    k = gk.reshape(B, L, GLA_HEADS, GLA_DK)
    v = gv.reshape(B, L, GLA_HEADS, GLA_DV)

    def gla_decay(lr, d):
        z = lr @ gla_w_lr[d] + gla_b_lr[d]
        g = jnp.maximum(jax.nn.log_sigmoid(z) / GLA_TAU, GLA_MIN_LOG_DECAY)
        return g.astype(jnp.float32).reshape(B, L, GLA_HEADS, GLA_DK)

    g_fwd = gla_decay(glr_f, 0)
    g_bwd = gla_decay(glr_b, 1)
    o_gla = gla_scan(q, k, v, g_fwd) + flip(gla_scan(flip(q), flip(k), flip(v), flip(g_bwd)))
    o_gla = head_rmsnorm(o_gla, gla_out_norm) * jax.nn.silu(gg).reshape(B, L, GLA_HEADS, GLA_DV)

    q = mq.reshape(B, L, MLSTM_HEADS, MLSTM_DK) * (MLSTM_DK ** -0.5)
    k = mk.reshape(B, L, MLSTM_HEADS, MLSTM_DK)
    v = mv.reshape(B, L, MLSTM_HEADS, MLSTM_DV)
    gates = (mgates.reshape(B, L, 4, MLSTM_HEADS) + mlstm_gate_b).astype(jnp.float32)
    log_i_fwd = gates[:, :, 0]
    log_f_fwd = jax.nn.log_sigmoid(gates[:, :, 1])
    log_i_bwd = gates[:, :, 2]
    log_f_bwd = jax.nn.log_sigmoid(gates[:, :, 3])
    h_m = (mlstm_scan(q, k, v, log_i_fwd, log_f_fwd)
           + flip(mlstm_scan(flip(q), flip(k), flip(v), flip(log_i_bwd), flip(log_f_bwd))))
    h_m = head_rmsnorm(h_m, mlstm_out_norm) * jax.nn.sigmoid(mo).reshape(B, L, MLSTM_HEADS, MLSTM_DV)

    merged = jnp.concatenate([o_gla.reshape(B, L, GLA_WIDTH), h_m.reshape(B, L, MLSTM_WIDTH)], axis=-1)
    return merged.astype(h.dtype) @ w_out


def mem_cross_attn(h, mem_n, wq, wk, wv, wo):
    B, L, _ = h.shape
    M = mem_n.shape[1]
    q = (h @ wq).reshape(B, L, XATTN_HEADS, XATTN_HEAD_DIM).astype(jnp.float32)
    k = (mem_n @ wk).reshape(B, M, XATTN_HEADS, XATTN_HEAD_DIM).astype(jnp.float32)
    v = (mem_n @ wv).reshape(B, M, XATTN_HEADS, XATTN_HEAD_DIM).astype(jnp.float32)
    s = jnp.einsum("blhd,bmhd->bhlm", q, k) * (XATTN_HEAD_DIM ** -0.5)
    p = jax.nn.softmax(s, axis=-1)
    o = jnp.einsum("bhlm,bmhd->blhd", p, v).reshape(B, L, XATTN_WIDTH).astype(h.dtype)
    return o @ wo


def trunk(x, mem,
          ffn1_norm, ffn1_w_gate, ffn1_w_up, ffn1_w_down,
          mix_norm, w_in, gla_w_lr, gla_b_lr, gla_out_norm, mlstm_gate_b, mlstm_out_norm, w_out,
          xattn_norm, mem_norm, xattn_wq, xattn_wk, xattn_wv, xattn_wo,
          ffn2_norm, ffn2_w_gate, ffn2_w_up, ffn2_w_down,
          final_norm):
    for l in range(DEPTH):
        x = x + 0.5 * swiglu(rmsnorm(x, ffn1_norm[l]), ffn1_w_gate[l], ffn1_w_up[l], ffn1_w_down[l])
        x = x + parallel_mixer(rmsnorm(x, mix_norm[l]), w_in[l], gla_w_lr[l], gla_b_lr[l],
                               gla_out_norm[l], mlstm_gate_b[l], mlstm_out_norm[l], w_out[l])
        x = x + mem_cross_attn(rmsnorm(x, xattn_norm[l]), rmsnorm(mem, mem_norm[l]),
                               xattn_wq[l], xattn_wk[l], xattn_wv[l], xattn_wo[l])
        x = x + 0.5 * swiglu(rmsnorm(x, ffn2_norm[l]), ffn2_w_gate[l], ffn2_w_up[l], ffn2_w_down[l])
    return rmsnorm(x, final_norm)


def setup_inputs(seed: int = 0) -> dict:
    key = jax.random.key(seed)
    ks = iter(jax.random.split(key, 64))

    def nrm(shape, scale):
        return jax.random.normal(next(ks), shape, jnp.float32) * scale

    def gain(shape):
        return 1.0 + nrm(shape, 0.02)

    D = D_MODEL
    gate_offset = (jnp.array([0.0, 1.0, 0.0, 1.0], jnp.float32)[:, None]
                   * jnp.linspace(3.0, 6.0, MLSTM_HEADS, dtype=jnp.float32)[None, :])
    return {
        "x_prompt": nrm((BATCH, SEQ, D), 1.0),
        "x_sample": nrm((DEC_BATCH, DEC_SEQ, D), 1.0),
        "mem_prompt": nrm((BATCH, MEM_TOKENS, D), 1.0),
        "mem_sample": nrm((DEC_BATCH, MEM_TOKENS, D), 1.0),
        "ffn1_norm": gain((DEPTH, D)),
        "ffn1_w_gate": nrm((DEPTH, D, D_FF), D ** -0.5),
        "ffn1_w_up": nrm((DEPTH, D, D_FF), D ** -0.5),
        "ffn1_w_down": nrm((DEPTH, D_FF, D), D_FF ** -0.5),
        "mix_norm": gain((DEPTH, D)),
        "w_in": nrm((DEPTH, D, IN_WIDTH), D ** -0.5),
        "gla_w_lr": nrm((DEPTH, 2, GLA_RANK, GLA_QK), GLA_RANK ** -0.5),
        "gla_b_lr": nrm((DEPTH, 2, GLA_QK), 0.1),
        "gla_out_norm": gain((DEPTH, GLA_WIDTH)),
        "mlstm_gate_b": gate_offset[None] + nrm((DEPTH, 4, MLSTM_HEADS), 0.1),
        "mlstm_out_norm": gain((DEPTH, MLSTM_WIDTH)),
        "w_out": nrm((DEPTH, MIX_WIDTH, D), MIX_WIDTH ** -0.5),
        "xattn_norm": gain((DEPTH, D)),
        "mem_norm": gain((DEPTH, D)),
        "xattn_wq": nrm((DEPTH, D, XATTN_WIDTH), D ** -0.5),
        "xattn_wk": nrm((DEPTH, D, XATTN_WIDTH), D ** -0.5),
        "xattn_wv": nrm((DEPTH, D, XATTN_WIDTH), D ** -0.5),
        "xattn_wo": nrm((DEPTH, XATTN_WIDTH, D), XATTN_WIDTH ** -0.5),
        "ffn2_norm": gain((DEPTH, D)),
        "ffn2_w_gate": nrm((DEPTH, D, D_FF), D ** -0.5),
        "ffn2_w_up": nrm((DEPTH, D, D_FF), D ** -0.5),
        "ffn2_w_down": nrm((DEPTH, D_FF, D), D_FF ** -0.5),
        "final_norm": gain((D,)),
    }


def reference(x_prompt, x_sample, mem_prompt, mem_sample,
              ffn1_norm, ffn1_w_gate, ffn1_w_up, ffn1_w_down,
              mix_norm, w_in, gla_w_lr, gla_b_lr, gla_out_norm, mlstm_gate_b, mlstm_out_norm, w_out,
              xattn_norm, mem_norm, xattn_wq, xattn_wk, xattn_wv, xattn_wo,
              ffn2_norm, ffn2_w_gate, ffn2_w_up, ffn2_w_down,
              final_norm):
    y_prompt = trunk(x_prompt, mem_prompt,
                     ffn1_norm, ffn1_w_gate, ffn1_w_up, ffn1_w_down,
                     mix_norm, w_in, gla_w_lr, gla_b_lr, gla_out_norm, mlstm_gate_b, mlstm_out_norm, w_out,
                     xattn_norm, mem_norm, xattn_wq, xattn_wk, xattn_wv, xattn_wo,
                     ffn2_norm, ffn2_w_gate, ffn2_w_up, ffn2_w_down,
                     final_norm)
    y_sample = trunk(x_sample, mem_sample,
                     ffn1_norm, ffn1_w_gate, ffn1_w_up, ffn1_w_down,
                     mix_norm, w_in, gla_w_lr, gla_b_lr, gla_out_norm, mlstm_gate_b, mlstm_out_norm, w_out,
                     xattn_norm, mem_norm, xattn_wq, xattn_wk, xattn_wv, xattn_wo,
                     ffn2_norm, ffn2_w_gate, ffn2_w_up, ffn2_w_down,
                     final_norm)
    return (y_prompt, y_sample)
```

```python
import contextlib
import os
import numpy as np
KDBG = int(os.environ.get('KDBG', '0'))
import concourse.bass as bass
import concourse.mybir as mybir
from concourse.bass_utils import run_bass_kernel_spmd

F32 = mybir.dt.float32
BF16 = mybir.dt.bfloat16
AF = mybir.ActivationFunctionType
ALU = mybir.AluOpType

D = 4096
KC = D // 128
MEMT = 256
EPS = 1e-6
SEM_CAP = 30000
GQ, GK, GV, GG, GLRF, GLRB, MQ, MK, MV, MO, MG, INW = (
    0, 1024, 2048, 4096, 6144, 6160, 6176, 7200, 8224, 10272, 12320, 12336)

C_ID, C_MF, C_MB, C_MBS, C_MFS, C_IND, C_ONE, C_EPS, C_ZERO, NCONST = 0, 128, 256, 384, 512, 640, 642, 643, 644, 644 + 256


def make_consts():
    i = np.arange(128)[:, None]
    j = np.arange(128)[None, :]
    same = (i // 64) == (j // 64)
    c = np.zeros((128, NCONST), np.float32)
    c[:, C_ID:C_ID + 128] = (i == j)
    c[:, C_MF:C_MF + 128] = same & (i <= j)
    c[:, C_MB:C_MB + 128] = same & (i >= j)
    c[:, C_MBS:C_MBS + 128] = same & (i > j)
    c[:, C_MFS:C_MFS + 128] = same & (i < j)
    c[:, C_IND] = (np.arange(128) < 64)
    c[:, C_IND + 1] = (np.arange(128) >= 64)
    c[:, C_ONE] = 1.0
    c[:, C_EPS] = EPS
    return c


class Buf:
    __slots__ = ("name", "w", "r", "al", "const")

    def __init__(self, name, const=False):
        self.name = name
        self.w = None
        self.r = {}
        self.al = [self]
        self.const = const


class Em:
    COMPUTE = ("pe", "act", "dve", "pool")

    def __init__(self, nc, es):
        self.nc, self.es = nc, es
        self.E = {"pe": nc.tensor, "act": nc.scalar, "dve": nc.vector, "pool": nc.gpsimd, "sp": nc.sync}
        self.nsem = 0
        self.csem = {e: self._newsem() for e in self.COMPUTE}
        self.ccnt = {e: 0 for e in self.COMPUTE}
        self.seq = {e: 0 for e in self.COMPUTE}
        self.seen = {e: {} for e in self.E}
        self.dring = {"sp": [[self._newsem(), 0] for _ in range(16)],
                      "pool": [[self._newsem(), 0] for _ in range(8)]}
        self.dpos = {"sp": 0, "pool": 0}
        self.ndma = 0
        self.oldsems = []

    def _newsem(self):
        self.nsem += 1
        return self.es.enter_context(self.nc.semaphore("s%d" % self.nsem))

    def _wait(self, eng, sig):
        sem, val = sig[0], sig[1]
        seen = self.seen[eng]
        k = id(sem)
        if seen.get(k, 0) >= val:
            return
        self.E[eng].wait_ge(sem, val)
        seen[k] = val

    def op(self, eng, fn, reads=(), writes=(), dma=False):
        deps = []
        for b in reads:
            for a in b.al:
                if a.w is not None:
                    deps.append(a.w)
        for b in writes:
            for a in b.al:
                if a.w is not None:
                    deps.append(a.w)
                if a.r:
                    deps.extend(a.r.values())
        for sig in deps:
            if (not dma) and sig[2] == eng:
                if eng == "pe":
                    continue
                if self.seq[eng] - sig[3] >= 6:
                    continue
            self._wait(eng, sig)
        if dma:
            ring = self.dring[eng]
            slot = ring[self.dpos[eng] % len(ring)]
            self.dpos[eng] += 1
            if slot[1] > 0:
                self._wait(eng, (slot[0], slot[1]))
            if slot[1] + 16 > SEM_CAP:
                self.oldsems.append((slot[0], slot[1]))
                slot[0] = self._newsem()
                slot[1] = 0
            ins = fn(self.E[eng])
            slot[1] += 16
            ins.then_inc(slot[0], 16)
            sig = (slot[0], slot[1], "dma", 0)
            self.ndma += 1
            key = ("d", self.ndma)
        else:
            if self.ccnt[eng] >= SEM_CAP:
                self.csem[eng] = self._newsem()
                self.ccnt[eng] = 0
            ins = fn(self.E[eng])
            self.ccnt[eng] += 1
            self.seq[eng] += 1
            ins.then_inc(self.csem[eng], 1)
            sig = (self.csem[eng], self.ccnt[eng], eng, self.seq[eng])
            key = eng
        for b in reads:
            if not b.const:
                b.r[key] = sig
        for b in writes:
            for a in b.al:
                a.w = sig
                a.r = {}
        return ins

    def barrier(self):
        sigs = [(self.csem[e], self.ccnt[e]) for e in self.COMPUTE if self.ccnt[e] > 0]
        for q in self.dring:
            sigs += [(s[0], s[1]) for s in self.dring[q] if s[1] > 0]
        for e in self.E:
            for sg in sigs:
                self._wait(e, sg)

    def finish(self):
        for q in self.dring:
            for s in self.dring[q]:
                if s[1] > 0:
                    self._wait("sp", (s[0], s[1]))


def build(L, DFF, DEPTH, stages=("ffn1", "mix", "xattn", "ffn2")):
    assert L % 128 == 0 and DFF % 256 == 0
    T = min(512, L)
    NSUB = T // 128
    NPASS = L // T
    FC = DFF // 128
    NPAIR = L // 128
    NCH = L // 64
    PB = min(8, NPAIR)

    nc = bass.Bass("TRN2", target_bir_lowering=False)
    es = contextlib.ExitStack()
    em = Em(nc, es)

    def din(name, shape):
        return nc.dram_tensor(name, list(shape), F32, kind="ExternalInput").ap()

    def dint(name, shape, dt):
        return nc.dram_tensor(name, list(shape), dt, kind="Internal").ap()

    x_in = din("x", (L, D))
    mem_in = din("mem", (MEMT, D))
    consts_in = din("consts", (128, NCONST))
    W = {}
    for nm, shp in (("ffn1_norm", (D,)), ("ffn1_w_gate", (D, DFF)), ("ffn1_w_up", (D, DFF)),
                    ("ffn1_w_down", (DFF, D)), ("mix_norm", (D,)), ("w_in", (D, INW)),
                    ("gla_w_lr", (2, 16, 1024)), ("gla_b_lr", (2, 1024)), ("gla_out_norm", (2048,)),
                    ("mlstm_gate_b", (16,)), ("mlstm_out_norm", (2048,)), ("w_out", (D, D)),
                    ("xattn_norm", (D,)), ("mem_norm", (D,)), ("xattn_wq", (D, 512)),
                    ("xattn_wk", (D, 512)), ("xattn_wv", (D, 512)), ("xattn_wo", (512, D)),
                    ("ffn2_norm", (D,)), ("ffn2_w_gate", (D, DFF)), ("ffn2_w_up", (D, DFF)),
                    ("ffn2_w_down", (DFF, D))):
        W[nm] = din(nm, (DEPTH,) + shp)
    final_norm = din("final_norm", (D,))
    y_out = nc.dram_tensor("y", [L, D], F32, kind="ExternalOutput").ap()

    xs = dint("xs", (L, D), F32)
    qd = [dint("qd%d" % i, (L, 2048), BF16) for i in range(2)]
    ki = [dint("ki%d" % i, (L, 2048), BF16) for i in range(2)]
    ke = [dint("ke%d" % i, (L, 2048), BF16) for i in range(2)]
    vv = dint("vv", (L, 4096), BF16)
    gate = dint("gate", (L, 4096), BF16)
    merged = dint("merged", (L, 4096), BF16)

    def dbufs(name, ncb):
        return [[Buf("%s_%d_%d" % (name, r, c)) for c in range(ncb)] for r in range(NPAIR)]
    B_xs = dbufs("xs", 8)
    B_xin = [[Buf("xin", const=True) for c in range(8)] for r in range(NPAIR)]
    B_y = dbufs("y", 8)
    B_qd = [dbufs("qd%d" % i, 4) for i in range(2)]
    B_ki = [dbufs("ki%d" % i, 4) for i in range(2)]
    B_ke = [dbufs("ke%d" % i, 4) for i in range(2)]
    B_vv = dbufs("vv", 8)
    B_gate = dbufs("gate", 8)
    B_mg = dbufs("mg", 8)
    B_w = Buf("weights", const=True)

    def sb(name, shape, dt):
        return es.enter_context(nc.sbuf_tensor(name, list(shape), dt))

    def ps(name, shape, dt):
        return es.enter_context(nc.psum_tensor(name, list(shape), dt))

    cst = sb("cst", (128, NCONST), F32)
    B_cst = Buf("cst", const=True)
    identb = sb("identb", (128, 128), BF16)
    onesb = sb("onesb", (128, 1), BF16)
    B_cb = Buf("cb", const=True)
    import types
    C = types.SimpleNamespace()
    core_es = [None]
    gen = [0]

    def alloc_core():
        gen[0] += 1
        ce = contextlib.ExitStack()
        core_es[0] = ce

        def sbc(name, shape, dt):
            return ce.enter_context(nc.sbuf_tensor("%s_g%d" % (name, gen[0]), list(shape), dt))
        C.XT = sbc("XT", (128, KC, T), BF16)
        C.B_XT = [[Buf("XT%d_%d" % (c, s)) for s in range(NSUB)] for c in range(KC)]
        C.wslot = [sbc("wslot%d" % i, (128, 4096), BF16) for i in range(NSLOT)]
        C.B_ws = [Buf("ws%d" % i) for i in range(NSLOT)]
        C.xtile = sbc("xtile", (128, D), F32)
        C.B_xtile = Buf("xtile")
        C.xb = sbc("xb", (128, D), BF16)
        C.B_xb = Buf("xb")

    def free_core():
        core_es[0].close()
    NSLOT = 4
    st4 = sb("st4", (128, 8), F32)
    B_ss, B_rstd = Buf("ss"), Buf("rstd")
    wcol = sb("wcol", (128, KC), F32)
    wrow = sb("wrow", (KC, 128), F32)
    B_wrow = Buf("wrow")
    B_wcol = Buf("wcol")
    accb = [ps("acc%d" % i, (128, 512), F32) for i in range(6)]
    B_acc = [Buf("acc%d" % i) for i in range(6)]
    trp = [ps("trp%d" % i, (128, 1024), BF16) for i in range(2)]
    B_tr = [Buf("tr%d" % i) for i in range(4)]
    state = {"acc": 0, "tr": 0, "ws": 0, "alt": 0}

    def acc_next():
        i = state["acc"] % 6
        state["acc"] += 1
        return accb[i], B_acc[i]

    def tr_next():
        i = state["tr"] % 2
        state["tr"] += 1
        return trp[i][:, 0:512], B_tr[i]

    def alt():
        state["alt"] += 1
        return "act" if state["alt"] % 2 else "dve"

    def copy_op(eng, out, in_, reads, writes):
        if eng == "act":
            em.op("act", lambda e: e.activation(out=out, in_=in_, func=AF.Copy), reads, writes)
        else:
            em.op("dve", lambda e: e.tensor_copy(out=out, in_=in_), reads, writes)

    em.op("sp", lambda e: e.dma_start(out=cst[:], in_=consts_in), [], [B_cst], dma=True)
    em.op("dve", lambda e: e.tensor_copy(out=identb[:], in_=cst[:, C_ID:C_ID + 128]), [B_cst], [B_cb])
    em.op("dve", lambda e: e.tensor_copy(out=onesb[:], in_=cst[:, C_ONE:C_ONE + 1]), [B_cst], [B_cb])
    B_cst.const = True
    one_col = cst[:, C_ONE:C_ONE + 1]
    eps_col = cst[:, C_EPS:C_EPS + 1]

    def load_wcol(w1d):
        em.op("sp", lambda e: e.dma_start(out=wrow[:], in_=w1d.rearrange("(c p) -> c p", p=128)), [], [B_wrow], dma=True)
        pst, Bps = acc_next()
        em.op("pe", lambda e: e.transpose(out=pst[:, 0:KC], in_=wrow[:], identity=cst[0:KC, C_ID:C_ID + KC]), [B_wrow, B_cst], [Bps])
        em.op("dve", lambda e: e.tensor_copy(out=wcol[:], in_=pst[:, 0:KC]), [Bps], [B_wcol])

    def rstd_from_ss(n):
        em.op("act", lambda e: e.activation(out=st4[:, 1:2], in_=st4[:, 0:1], func=AF.Sqrt, scale=1.0 / n, bias=eps_col),
              [B_ss, B_cst], [B_rstd])
        em.op("dve", lambda e: e.reciprocal(out=st4[:, 1:2], in_=st4[:, 1:2]), [B_rstd], [B_rstd])

    def transpose_to_XT(src_bf, B_src, nchunk, sub, scale_cols, xt=None, bxt=None, tok=128):
        xt = C.XT if xt is None else xt
        bxt = C.B_XT if bxt is None else bxt
        for c4 in range(0, nchunk, 4):
            n = min(4, nchunk - c4)
            tr, Btr = tr_next()

            def fn(e):
                for j in range(n):
                    ins = e.transpose(out=tr[:, j * 128:(j + 1) * 128],
                                      in_=src_bf[:, (c4 + j) * 128:(c4 + j + 1) * 128], identity=identb[:])
                return ins
            em.op("pe", fn, [B_src, B_cb], [Btr])
            if scale_cols:
                for j in range(n):
                    c = c4 + j
                    eng = "dve"
                    o = xt[:, c, sub * 128:(sub + 1) * 128]
                    i_ = tr[:, j * 128:(j + 1) * 128]
                    if eng == "act":
                        em.op("act", lambda e: e.activation(out=o, in_=i_, func=AF.Copy, scale=wcol[:, c:c + 1]),
                              [Btr, B_wcol], [bxt[c][sub]])
                    else:
                        em.op("dve", lambda e: e.tensor_scalar(out=o, in0=i_, scalar1=wcol[:, c:c + 1], scalar2=None,
                                                               op0=ALU.mult), [Btr, B_wcol], [bxt[c][sub]])
            else:
                o = xt[:, c4:c4 + n, sub * 128:(sub + 1) * 128]
                i_ = tr[:, 0:n * 128].rearrange("p (c t) -> p c t", t=128)
                copy_op(alt(), o, i_, [Btr], [bxt[c4 + j][sub] for j in range(n)])

    def norm_stage(src, Bsrc, row0, nsub):
        for sub in range(nsub):
            r0 = row0 + sub * 128
            em.op("sp", lambda e: e.dma_start(out=C.xtile[:], in_=src[r0:r0 + 128, :]), Bsrc[r0 // 128], [C.B_xtile], dma=True)
            em.op("act", lambda e: e.activation(out=C.xb[:], in_=C.xtile[:], func=AF.Square, accum_out=st4[:, 0:1]),
                  [C.B_xtile], [C.B_xb, B_ss])
            rstd_from_ss(D)
            em.op("dve", lambda e: e.tensor_scalar(out=C.xb[:], in0=C.xtile[:], scalar1=st4[:, 1:2], scalar2=None,
                                                   op0=ALU.mult), [C.B_xtile, B_rstd], [C.B_xb])
            transpose_to_XT(C.xb, C.B_xb, KC, sub, True)

    class WStream:
        def __init__(self):
            self.pending = []
            self.issued = []

        def plan(self, tiles):
            self.pending.extend(tiles)

        def _issue(self):
            wap, k0, kn, c0, cw = self.pending.pop(0)
            i = state["ws"] % NSLOT
            state["ws"] += 1
            view = C.wslot[i][:, :kn * cw].rearrange("p (k c) -> p k c", c=cw)
            srcap = wap[k0 * 128:(k0 + kn) * 128, c0:c0 + cw].rearrange("(k p) c -> p k c", p=128)
            em.op("pool", lambda e: e.dma_start(out=view, in_=srcap), [B_w], [C.B_ws[i]], dma=True)
            self.issued.append((view, C.B_ws[i]))

        def get(self):
            while self.pending and len(self.issued) < NSLOT - 1:
                self._issue()
            if not self.issued:
                self._issue()
            r = self.issued.pop(0)
            while self.pending and len(self.issued) < NSLOT - 1:
                self._issue()
            return r

    ws = WStream()

    def linear(xt, bxt, K, nsub, blocks):
        kper = {}
        for blk_ in blocks:
            parts = blk_[0]
            for (wap, c0, cw, bc) in parts:
                kp = max(1, min(K, 4096 // cw))
                for k0 in range(0, K, kp):
                    ws.plan([(wap, k0, min(kp, K - k0), c0, cw)])
        deferred = []
        for bi, blk_ in enumerate(blocks):
            parts, epi = blk_[0], blk_[1]
            post = blk_[2] if len(blk_) > 2 else None
            banks = [acc_next() for _ in range(nsub)]
            first_tile = True
            for (wap, c0, cw, bc) in parts:
                kp = max(1, min(K, 4096 // cw))
                for k0 in range(0, K, kp):
                    kn = min(kp, K - k0)
                    wt, Bw = ws.get()
                    if KDBG == 2:
                        continue
                    for sub in range(nsub):
                        def fn(e):
                            for k in range(kn):
                                ins = e.matmul(banks[sub][0][:, bc:bc + cw], lhsT=xt[:, k0 + k, sub * 128:(sub + 1) * 128],
                                               rhs=wt[:, k, :], start=(k0 + k == 0), stop=(k0 + k == K - 1))
                            return ins
                        em.op("pe", fn, [Bw] + [bxt[k0 + k][sub] for k in range(kn)], [banks[sub][1]])
                    if first_tile:
                        first_tile = False
                        for f2 in deferred:
                            f2()
                        deferred = []
            for sub in range(nsub):
                if KDBG == 2 or KDBG == 3:
                    continue
                r2 = epi(bi, sub, banks[sub][0], banks[sub][1])
                if r2 is not None:
                    deferred.append(r2)
            if post is not None:
                for sub in range(nsub):
                    post(bi, sub)
        for f2 in deferred:
            f2()

    epf = [sb("epf%d" % i, (128, 512), F32) for i in range(3)]
    B_epf = [Buf("epf%d" % i) for i in range(3)]
    epb = [sb("epb%d" % i, (128, 512), BF16) for i in range(6)]
    B_epb = [Buf("epb%d" % i) for i in range(6)]
    rot = {"f": 0, "b": 0}

    dec = sb("dec", (128, 4 * 8, NCH), F32)
    B_dec = Buf("dec")
    alloc_core()

    def epf_next():
        i = rot["f"] % 3
        rot["f"] += 1
        return epf[i], B_epf[i]

    def epb_next():
        i = rot["b"] % 6
        rot["b"] += 1
        return epb[i], B_epb[i]

    def resid_epilogue(src, Bsrc, dst, Bdst, row0, scale):
        def epi(cb, sub, pst, Bps):
            r0 = row0 + sub * 128
            t, Bt = epf_next()
            em.op("sp", lambda e: e.dma_start(out=t[:], in_=src[r0:r0 + 128, cb * 512:(cb + 1) * 512]),
                  [Bsrc[r0 // 128][cb]], [Bt], dma=True)
            em.op("dve", lambda e: e.scalar_tensor_tensor(out=t[:], in0=pst[:, :512], scalar=scale, in1=t[:],
                                                          op0=ALU.mult, op1=ALU.add), [Bps, Bt], [Bt])
            em.op("sp", lambda e: e.dma_start(out=dst[r0:r0 + 128, cb * 512:(cb + 1) * 512], in_=t[:]),
                  [Bt], [Bdst[r0 // 128][cb]], dma=True)
        return epi

    def ffn(l, pre, src, Bsrc):
        fst = contextlib.ExitStack()
        HT = fst.enter_context(nc.sbuf_tensor("HT_%s%d" % (pre, l), [128, FC, T], BF16))
        B_HT = [[Buf("HT%d_%d" % (c, s)) for s in range(NSUB)] for c in range(FC)]
        load_wcol(W[pre + "_norm"][l])
        wg, wu, wd = W[pre + "_w_gate"][l], W[pre + "_w_up"][l], W[pre + "_w_down"][l]
        for p in range(NPASS):
            row0 = p * T
            if KDBG == 11:
                continue
            norm_stage(src if p >= 0 else src, Bsrc, row0, NSUB)
            if KDBG in (1, 13):
                continue

            def epiA(cb, sub, pst, Bps):
                t, Bt = epf_next()
                em.op("act", lambda e: e.activation(out=t[:, :256], in_=pst[:, 0:256], func=AF.Silu), [Bps], [Bt])
                hb, Bh = epb_next()
                em.op("dve", lambda e: e.tensor_tensor(out=hb[:, :256], in0=t[:, :256], in1=pst[:, 256:512], op=ALU.mult),
                      [Bt, Bps], [Bh])
                return lambda: transpose_to_XT(hb, Bh, 2, sub, False, xt=HT[:, 2 * cb:2 * cb + 2, :], bxt=B_HT[2 * cb:2 * cb + 2])
            blocksA = [([(wg, cb * 256, 256, 0), (wu, cb * 256, 256, 256)], epiA) for cb in range(DFF // 256)]
            linear(C.XT, C.B_XT, KC, NSUB, blocksA)
            epiB = resid_epilogue(src, Bsrc, xs, B_xs, row0, 0.5)
            blocksB = [([(wd, cb * 512, 512, 0)], epiB) for cb in range(8)]
            linear(HT, B_HT, FC, NSUB, blocksB)
        em.barrier()
        fst.close()

    def final_stage(src, Bsrc):
        fin = contextlib.ExitStack()
        wbc = fin.enter_context(nc.sbuf_tensor("fn_wbc", [128, D], F32))
        B_wbc = Buf("wbc")
        em.op("sp", lambda e: e.dma_start(out=wbc[:], in_=final_norm.partition_broadcast(128)), [], [B_wbc], dma=True)
        for r in range(NPAIR):
            em.op("sp", lambda e: e.dma_start(out=C.xtile[:], in_=src[r * 128:(r + 1) * 128, :]), Bsrc[r], [C.B_xtile], dma=True)
            em.op("act", lambda e: e.activation(out=C.xb[:], in_=C.xtile[:], func=AF.Square, accum_out=st4[:, 0:1]),
                  [C.B_xtile], [C.B_xb, B_ss])
            rstd_from_ss(D)
            em.op("dve", lambda e: e.scalar_tensor_tensor(out=C.xtile[:], in0=C.xtile[:], scalar=st4[:, 1:2], in1=wbc[:],
                                                          op0=ALU.mult, op1=ALU.mult), [C.B_xtile, B_rstd, B_wbc], [C.B_xtile])
            em.op("sp", lambda e: e.dma_start(out=y_out[r * 128:(r + 1) * 128, :], in_=C.xtile[:]), [C.B_xtile], B_y[r], dma=True)
        return fin

    def mixer(l, src, Bsrc):
        w_in = W["w_in"][l]
        m1 = contextlib.ExitStack()

        def sbm(name, shape, dt):
            return m1.enter_context(nc.sbuf_tensor(name + "_%d" % l, list(shape), dt))
        fac = sbm("fac", (128, NSUB * 6, 1024), BF16)
        B_fac = [[Buf("fac%d_%d" % (s, d)) for d in range(2)] for s in range(NSUB)]
        wsm = sbm("wsm", (128, KC, 48), BF16)
        B_wsm = Buf("wsm")
        wlr = sbm("wlr", (16, 2, 1024), BF16)
        blr = sbm("blr", (128, 2, 1024), F32)
        gbias = sbm("gbias", (128, 16), F32)
        B_misc = Buf("m1misc")
        lrT = sbm("lrT", (16, 2, T), BF16)
        B_lrT = Buf("lrT")
        gts = sbm("gts", (128, NSUB, 16), F32)
        lsg = sbm("lsg", (128, NSUB, 16), F32)
        eig = sbm("eig", (128, NSUB, 16), F32)
        B_gts = [Buf("gts%d" % s) for s in range(NSUB)]
        gt = [sbm("gt%d" % i, (128, 1024), F32) for i in range(2)]
        B_gt = [Buf("gt%d" % i) for i in range(2)]
        ztmp = [sbm("zt%d" % i, (128, 512), F32) for i in range(2)]
        B_zt = [Buf("zt%d" % i) for i in range(2)]
        rr = {"gt": 0, "zt": 0}

        em.op("pool", lambda e: e.dma_start(out=wsm[:, :, 0:32], in_=w_in[:, GLRF:GLRF + 32].rearrange("(k p) c -> p k c", p=128)),
              [], [B_wsm], dma=True)
        em.op("pool", lambda e: e.dma_start(out=wsm[:, :, 32:48], in_=w_in[:, MG:MG + 16].rearrange("(k p) c -> p k c", p=128)),
              [], [B_wsm], dma=True)
        em.op("pool", lambda e: e.dma_start(out=wlr[:], in_=W["gla_w_lr"][l].rearrange("d r c -> r d c")), [], [B_misc], dma=True)
        em.op("sp", lambda e: e.dma_start(out=blr[:].rearrange("p d c -> p (d c)"),
                                          in_=W["gla_b_lr"][l].rearrange("d c -> (d c)").partition_broadcast(128)),
              [], [B_misc], dma=True)
        em.op("sp", lambda e: e.dma_start(out=gbias[:], in_=W["mlstm_gate_b"][l].partition_broadcast(128)), [], [B_misc], dma=True)
        load_wcol(W["mix_norm"][l])

        for p in range(NPASS):
            row0 = p * T
            norm_stage(src, Bsrc, row0, NSUB)
            for d_ in range(2):
                pst, Bps = acc_next()

                def fn(e):
                    for k in range(KC):
                        ins = e.matmul(pst[0:16, :T], lhsT=wsm[:, k, d_ * 16:(d_ + 1) * 16], rhs=C.XT[:, k, :],
                                       start=(k == 0), stop=(k == KC - 1))
                    return ins
                em.op("pe", fn, [B_wsm] + [C.B_XT[k][s] for k in range(KC) for s in range(NSUB)], [Bps])
                copy_op("act", lrT[:, d_, :], pst[0:16, :T], [Bps], [B_lrT])
            for sub in range(NSUB):
                pst, Bps = acc_next()

                def fn(e):
                    for k in range(KC):
                        ins = e.matmul(pst[:, 0:16], lhsT=C.XT[:, k, sub * 128:(sub + 1) * 128], rhs=wsm[:, k, 32:48],
                                       start=(k == 0), stop=(k == KC - 1))
                    return ins
                em.op("pe", fn, [B_wsm] + [C.B_XT[k][sub] for k in range(KC)], [Bps])
                em.op("dve", lambda e: e.tensor_tensor(out=gts[:, sub, :], in0=pst[:, 0:16], in1=gbias[:], op=ALU.add),
                      [Bps, B_misc], [B_gts[sub]])
                em.op("act", lambda e: e.activation(out=eig[:, sub, :], in_=gts[:, sub, :], func=AF.Exp), [B_gts[sub]], [B_gts[sub]])
                em.op("act", lambda e: e.activation(out=lsg[:, sub, :], in_=gts[:, sub, :], func=AF.Exp, scale=-1.0),
                      [B_gts[sub]], [B_gts[sub]])
                em.op("act", lambda e: e.activation(out=lsg[:, sub, :], in_=lsg[:, sub, :], func=AF.Ln, bias=one_col),
                      [B_gts[sub]], [B_gts[sub]])
                em.op("dve", lambda e: e.tensor_scalar(out=lsg[:, sub, :], in0=lsg[:, sub, :], scalar1=-1.0, scalar2=None,
                                                       op0=ALU.mult), [B_gts[sub]], [B_gts[sub]])

            def factors(kind):
                for sub in range(NSUB):
                    for d_ in range(2):
                        g = gt[rr["gt"] % 2]
                        Bg = B_gt[rr["gt"] % 2]
                        rr["gt"] += 1
                        if kind == 0:
                            for half in range(2):
                                pst, Bps = acc_next()
                                em.op("pe", lambda e: e.matmul(pst[:, :512], lhsT=lrT[:, d_, sub * 128:(sub + 1) * 128],
                                                               rhs=wlr[:, d_, half * 512:(half + 1) * 512], start=True, stop=True),
                                      [B_lrT, B_misc], [Bps])
                                z = ztmp[rr["zt"] % 2]
                                Bz = B_zt[rr["zt"] % 2]
                                rr["zt"] += 1
                                em.op("dve", lambda e: e.tensor_tensor(out=z[:], in0=pst[:, :512], in1=blr[:, d_, half * 512:(half + 1) * 512],
                                                                       op=ALU.add), [Bps, B_misc], [Bz])
                                em.op("act", lambda e: e.activation(out=z[:], in_=z[:], func=AF.Exp, scale=-1.0), [Bz], [Bz])
                                em.op("act", lambda e: e.activation(out=z[:], in_=z[:], func=AF.Ln, bias=one_col), [Bz], [Bz])
                                em.op("dve", lambda e: e.tensor_scalar(out=g[:, half * 512:(half + 1) * 512], in0=z[:], scalar1=-1.0 / 16.0,
                                                                       scalar2=-1.0, op0=ALU.mult, op1=ALU.max), [Bz], [Bg])
                        else:
                            for h in range(4):
                                col = 4 + 8 * d_ + h
                                em.op("dve", lambda e: e.tensor_scalar(out=g[:, h * 256:(h + 1) * 256], in0=cst[:, C_ZERO:C_ZERO + 256],
                                                                       scalar1=lsg[:, sub, col:col + 1], scalar2=None, op0=ALU.add),
                                      [B_gts[sub], B_cst], [Bg])
                        triP = cst[:, C_MF:C_MF + 128] if d_ == 0 else cst[:, C_MB:C_MB + 128]
                        triE = cst[:, C_MBS:C_MBS + 128] if d_ == 0 else cst[:, C_MFS:C_MFS + 128]
                        fb = sub * 6 + d_ * 3
                        for half in range(2):
                            hs = slice(half * 512, (half + 1) * 512)
                            pP, BpP = acc_next()
                            em.op("pe", lambda e: e.matmul(pP[:, :512], lhsT=triP, rhs=g[:, hs], start=True, stop=True), [Bg, B_cst], [BpP])
                            em.op("act", lambda e: e.activation(out=fac[:, fb + 0, hs], in_=pP[:, :512], func=AF.Exp), [BpP], [B_fac[sub][d_]])
                            em.op("act", lambda e: e.activation(out=fac[:, fb + 1, hs], in_=pP[:, :512], func=AF.Exp, scale=-1.0),
                                  [BpP], [B_fac[sub][d_]])
                            pE, BpE = acc_next()
                            em.op("pe", lambda e: e.matmul(pE[:, :512], lhsT=triE, rhs=g[:, hs], start=True, stop=True), [Bg, B_cst], [BpE])
                            em.op("act", lambda e: e.activation(out=fac[:, fb + 2, hs], in_=pE[:, :512], func=AF.Exp), [BpE], [B_fac[sub][d_]])
                        pB, BpB = acc_next()

                        def fnb(e):
                            for dc in range(8):
                                ins = e.matmul(pB[:, dc * 2:dc * 2 + 2], lhsT=g[:, dc * 128:(dc + 1) * 128], rhs=cst[:, C_IND:C_IND + 2],
                                               start=True, stop=True)
                            return ins
                        em.op("pe", fnb, [Bg, B_cst], [BpB])
                        ci = (row0 + sub * 128) // 64
                        base = (kind * 2 + d_) * 8
                        em.op("act", lambda e: e.activation(out=dec[:, base:base + 8, ci:ci + 2],
                                                            in_=pB[:, 0:16].rearrange("p (a b) -> p a b", b=2), func=AF.Exp),
                              [BpB], [B_dec])

            def store(t, Bt, dst, Bdst, r0, c0):
                em.op("sp", lambda e: e.dma_start(out=dst[r0:r0 + 128, c0:c0 + 512], in_=t[:]), [Bt], [Bdst[r0 // 128][c0 // 512]], dma=True)

            def epi_q(kind, qb):
                def epi(bi, sub, pst, Bps):
                    r0 = row0 + sub * 128
                    for d_ in range(2):
                        t, Bt = epb_next()
                        em.op("dve", lambda e: e.scalar_tensor_tensor(out=t[:], in0=pst[:, :512], scalar=1.0 / 16.0,
                                                                      in1=fac[:, sub * 6 + d_ * 3 + 0, qb * 512:(qb + 1) * 512],
                                                                      op0=ALU.mult, op1=ALU.mult), [Bps, B_fac[sub][d_]], [Bt])
                        store(t, Bt, qd[d_], B_qd[d_], r0, kind * 1024 + qb * 512)
                return epi

            def epi_k(kind, kb):
                def epi(bi, sub, pst, Bps):
                    r0 = row0 + sub * 128
                    for d_ in range(2):
                        for which, (dst, Bdst) in ((1, (ki[d_], B_ki[d_])), (2, (ke[d_], B_ke[d_]))):
                            t, Bt = epb_next()
                            for hh in range(2):
                                h = kb * 2 + hh
                                cs = slice(hh * 256, (hh + 1) * 256)
                                fsl = fac[:, sub * 6 + d_ * 3 + which, kb * 512 + hh * 256: kb * 512 + (hh + 1) * 256]
                                if kind == 0:
                                    em.op("dve", lambda e: e.tensor_tensor(out=t[:, cs], in0=pst[:, cs], in1=fsl, op=ALU.mult),
                                          [Bps, B_fac[sub][d_]], [Bt])
                                else:
                                    col = 8 * d_ + h
                                    em.op("dve", lambda e: e.scalar_tensor_tensor(out=t[:, cs], in0=pst[:, cs], scalar=eig[:, sub, col:col + 1],
                                                                                  in1=fsl, op0=ALU.mult, op1=ALU.mult),
                                          [Bps, B_fac[sub][d_], B_gts[sub]], [Bt])
                            store(t, Bt, dst, Bdst, r0, kind * 1024 + kb * 512)
                return epi

            def epi_act(func, dst, Bdst, c0):
                def epi(bi, sub, pst, Bps):
                    r0 = row0 + sub * 128
                    t, Bt = epb_next()
                    em.op("act", lambda e: e.activation(out=t[:], in_=pst[:, :512], func=func), [Bps], [Bt])
                    store(t, Bt, dst, Bdst, r0, c0)
                return epi

            for kind, (cq, ck, cv, cg, gfunc) in enumerate(((GQ, GK, GV, GG, AF.Silu), (MQ, MK, MV, MO, AF.Sigmoid))):
                factors(kind)
                blocks = []
                for b in range(2):
                    blocks.append(([(w_in, cq + b * 512, 512, 0)], epi_q(kind, b)))
                for b in range(2):
                    blocks.append(([(w_in, ck + b * 512, 512, 0)], epi_k(kind, b)))
                for b in range(4):
                    blocks.append(([(w_in, cv + b * 512, 512, 0)], epi_act(AF.Copy, vv, B_vv, kind * 2048 + b * 512)))
                for b in range(4):
                    blocks.append(([(w_in, cg + b * 512, 512, 0)], epi_act(gfunc, gate, B_gate, kind * 2048 + b * 512)))
                linear(C.XT, C.B_XT, KC, NSUB, blocks)
        em.barrier()
        m1.close()
        free_core()

        m2 = contextlib.ExitStack()

        def sb2(name, shape, dt):
            return m2.enter_context(nc.sbuf_tensor(name + "_%d" % l, list(shape), dt))
        oacc = sb2("oacc", (128, NPAIR, 512), F32)
        B_oacc = [Buf("oacc%d" % i) for i in range(NPAIR)]
        QD = [[sb2("QD%d_%d" % (d, i), (128, PB, 256), BF16) for i in range(2)] for d in range(2)]
        KI = [[sb2("KI%d_%d" % (d, i), (128, PB, 256), BF16) for i in range(2)] for d in range(2)]
        KE = [[sb2("KE%d_%d" % (d, i), (128, PB, 256), BF16) for i in range(2)] for d in range(2)]
        VV = [[sb2("VV%d_%d" % (d, i), (128, PB, 512), BF16) for i in range(2)] for d in range(2)]
        B_blk = [[Buf("blk%d_%d" % (d, i)) for i in range(2)] for d in range(2)]
        S32 = [sb2("S32_%d" % d, (128, 2, 512), F32) for d in range(2)]
        Sb = [sb2("Sb_%d" % d, (128, 2, 512), BF16) for d in range(2)]
        n32 = [sb2("n32_%d" % d, (128, 2), F32) for d in range(2)]
        nb = [sb2("nb_%d" % d, (128, 2), BF16) for d in range(2)]
        B_S = [[Buf("S32_%d_%d" % (d, dc)) for dc in range(2)] for d in range(2)]
        B_Sb = [[Buf("Sb_%d_%d" % (d, dc)) for dc in range(2)] for d in range(2)]
        B_n = [Buf("n32_%d" % d) for d in range(2)]
        B_nb = [Buf("nb_%d" % d) for d in range(2)]
        ATb = [[sb2("ATb%d_%d" % (d, i), (128, 128), BF16) for i in range(2)] for d in range(2)]
        B_AT = [[Buf("AT%d_%d" % (d, i)) for i in range(2)] for d in range(2)]
        qkT = [[sb2("qkT%d_%d" % (d, i), (128, 512), BF16) for i in range(2)] for d in range(2)]
        B_qkT = [[Buf("qkT%d_%d" % (d, i)) for i in range(2)] for d in range(2)]
        rden = [sb2("rden%d" % d, (128, 4), F32) for d in range(2)]
        B_rden = [Buf("rden%d" % d) for d in range(2)]
        nwbc = sb2("nwbc", (128, 512), F32)
        B_nw = Buf("nwbc")
        gtile = [sb2("gtile%d" % i, (128, 512), BF16) for i in range(2)]
        B_gtile = [Buf("gtile%d" % i) for i in range(2)]
        par = {"g": 0, "s": 0}
        NBLK = NPAIR // PB

        for kind in range(2):
            for h in range(4):
                ch0 = kind * 1024 + h * 256
                v0 = kind * 2048 + h * 512
                for d_ in range(2):
                    em.op("dve", lambda e: e.memset(S32[d_][:], 0.0), [], B_S[d_])
                    em.op("dve", lambda e: e.memset(Sb[d_][:], 0.0), [], B_Sb[d_])
                    if kind == 1:
                        em.op("dve", lambda e: e.memset(n32[d_][:], 0.0), [], [B_n[d_]])
                        em.op("dve", lambda e: e.memset(nb[d_][:], 0.0), [], [B_nb[d_]])

                def load_block(d_, blk, bi):
                    rows = slice(blk * PB * 128, (blk + 1) * PB * 128)
                    rt = range(blk * PB, (blk + 1) * PB)
                    for (dstt, srcd, Bd, c0, cw) in ((QD[d_][bi], qd[d_], B_qd[d_], ch0, 256), (KI[d_][bi], ki[d_], B_ki[d_], ch0, 256),
                                                     (KE[d_][bi], ke[d_], B_ke[d_], ch0, 256), (VV[d_][bi], vv, B_vv, v0, 512)):
                        em.op("sp", lambda e: e.dma_start(out=dstt[:], in_=srcd[rows, c0:c0 + cw].rearrange("(r p) c -> p r c", p=128)),
                              [Bd[r][c0 // 512] for r in rt], [B_blk[d_][bi]], dma=True)

                for step in range(NPAIR):
                    first_visit = step < NPAIR // 2
                    for d_ in range(2):
                        pg = step if d_ == 0 else NPAIR - 1 - step
                        blk = pg // PB
                        prl = pg % PB
                        bseq = step // PB
                        bi = bseq % 2
                        if step % PB == 0:
                            if step == 0:
                                load_block(d_, blk, bi)
                            nblk = blk + 1 if d_ == 0 else blk - 1
                            if 0 <= nblk < NBLK:
                                load_block(d_, nblk, 1 - bi)
                        pi = step % 2
                        Bblk = B_blk[d_][bi]
                        QDt, KIt, KEt, VVt = QD[d_][bi], KI[d_][bi], KE[d_][bi], VV[d_][bi]
                        qk, Bqk = qkT[d_][pi], B_qkT[d_][pi]
                        AT, BAT = ATb[d_][pi], B_AT[d_][pi]
                        smb, Bsm = accb[d_], B_acc[d_]
                        pO, BpO = accb[2 + d_], B_acc[2 + d_]
                        tr, Btr = tr_next()

                        def fnt(e):
                            for j in range(2):
                                e.transpose(out=tr[:, j * 128:(j + 1) * 128], in_=QDt[:, prl, j * 128:(j + 1) * 128], identity=identb[:])
                            for j in range(2):
                                ins = e.transpose(out=tr[:, 256 + j * 128:256 + (j + 1) * 128], in_=KIt[:, prl, j * 128:(j + 1) * 128],
                                                  identity=identb[:])
                            return ins
                        em.op("pe", fnt, [Bblk, B_cb], [Btr])
                        copy_op("act", qk[:], tr[:, 0:512], [Btr], [Bqk])

                        def fna(e):
                            e.matmul(smb[:, 0:128], lhsT=qk[:, 256:384], rhs=qk[:, 0:128], start=True, stop=False)
                            return e.matmul(smb[:, 0:128], lhsT=qk[:, 384:512], rhs=qk[:, 128:256], start=False, stop=True)
                        em.op("pe", fna, [Bqk], [Bsm])
                        mk = cst[:, C_MF:C_MF + 128] if d_ == 0 else cst[:, C_MB:C_MB + 128]
                        em.op("dve", lambda e: e.tensor_tensor(out=AT[:], in0=smb[:, 0:128], in1=mk, op=ALU.mult),
                              [Bsm, B_cst], [BAT])
                        if kind == 1:
                            em.op("pe", lambda e: e.matmul(smb[:, 130:131], lhsT=AT[:], rhs=onesb[:, 0:1], start=True, stop=True),
                                  [BAT, B_cb], [Bsm])
                        corder = (0, 1) if d_ == 0 else (1, 0)
                        dbase = (kind * 2 + d_) * 8 + h * 2
                        for cix, c in enumerate(corder):
                            cr = slice(c * 64, (c + 1) * 64)
                            chunk = pg * 2 + c

                            def fno(e):
                                for dc in range(2):
                                    ins = e.matmul(pO[cr, :512], lhsT=qk[:, dc * 128 + c * 64: dc * 128 + (c + 1) * 64],
                                                   rhs=Sb[d_][:, dc, :], start=(dc == 0), stop=False)
                                if cix == 1:
                                    ins = e.matmul(pO[:, :512], lhsT=AT[:], rhs=VVt[:, prl, :], start=False, stop=True)
                                return ins
                            em.op("pe", fno, [BAT, Bblk, Bqk] + B_Sb[d_], [BpO])
                            if kind == 1:
                                def fnd(e):
                                    for dc in range(2):
                                        ins = e.matmul(smb[cr, 131:132], lhsT=qk[:, dc * 128 + c * 64: dc * 128 + (c + 1) * 64],
                                                       rhs=nb[d_][:, dc:dc + 1], start=(dc == 0), stop=(dc == 1))
                                    return ins
                                em.op("pe", fnd, [Bqk, B_nb[d_]], [Bsm])
                            for dc in range(2):
                                sidx = 4 + (par["s"] % 2)
                                par["s"] += 1
                                pS, BpS = accb[sidx], B_acc[sidx]
                                em.op("pe", lambda e: e.matmul(pS[:, :512], lhsT=KEt[cr, prl, dc * 128:(dc + 1) * 128], rhs=VVt[cr, prl, :],
                                                               start=True, stop=True), [Bblk], [BpS])
                                em.op("dve", lambda e: e.scalar_tensor_tensor(out=S32[d_][:, dc, :], in0=S32[d_][:, dc, :],
                                                                              scalar=dec[:, dbase + dc, chunk:chunk + 1], in1=pS[:, :512],
                                                                              op0=ALU.mult, op1=ALU.add), [B_S[d_][dc], B_dec, BpS], [B_S[d_][dc]])
                                copy_op("act", Sb[d_][:, dc, :], S32[d_][:, dc, :], [B_S[d_][dc]], [B_Sb[d_][dc]])
                            if kind == 1:
                                def fnn(e):
                                    for dc in range(2):
                                        ins = e.matmul(smb[:, 128 + dc:128 + dc + 1], lhsT=KEt[cr, prl, dc * 128:(dc + 1) * 128], rhs=onesb[cr, 0:1],
                                                       start=True, stop=True)
                                    return ins
                                em.op("pe", fnn, [Bblk, B_cb], [Bsm])
                                for dc in range(2):
                                    em.op("dve", lambda e: e.scalar_tensor_tensor(out=n32[d_][:, dc:dc + 1], in0=n32[d_][:, dc:dc + 1],
                                                                                  scalar=dec[:, dbase + dc, chunk:chunk + 1],
                                                                                  in1=smb[:, 128 + dc:128 + dc + 1],
                                                                                  op0=ALU.mult, op1=ALU.add), [B_n[d_], B_dec, Bsm], [B_n[d_]])
                                em.op("dve", lambda e: e.tensor_copy(out=nb[d_][:], in_=n32[d_][:]), [B_n[d_]], [B_nb[d_]])
                        if kind == 0:
                            if first_visit:
                                copy_op("act", oacc[:, pg, :], pO[:, :512], [BpO], [B_oacc[pg]])
                            else:
                                em.op("dve", lambda e: e.tensor_tensor(out=oacc[:, pg, :], in0=pO[:, :512], in1=oacc[:, pg, :], op=ALU.add),
                                      [BpO, B_oacc[pg]], [B_oacc[pg]])
                        else:
                            rd, Brd = rden[d_], B_rden[d_]
                            em.op("dve", lambda e: e.tensor_copy(out=rd[:, 0:1], in_=smb[:, 130:131]), [Bsm], [Brd])
                            em.op("dve", lambda e: e.tensor_tensor(out=rd[:, 0:1], in0=smb[:, 131:132], in1=rd[:, 0:1], op=ALU.add),
                                  [Bsm, Brd], [Brd])
                            em.op("dve", lambda e: e.tensor_scalar(out=rd[:, 1:2], in0=rd[:, 0:1], scalar1=-1.0, scalar2=1.0,
                                                                   op0=ALU.mult, op1=ALU.max), [Brd], [Brd])
                            em.op("dve", lambda e: e.tensor_tensor(out=rd[:, 1:2], in0=rd[:, 0:1], in1=rd[:, 1:2], op=ALU.max),
                                  [Brd], [Brd])
                            em.op("dve", lambda e: e.reciprocal(out=rd[:, 2:3], in_=rd[:, 1:2]), [Brd], [Brd])
                            if first_visit:
                                em.op("dve", lambda e: e.tensor_scalar(out=oacc[:, pg, :], in0=pO[:, :512], scalar1=rd[:, 2:3], scalar2=None,
                                                                       op0=ALU.mult), [BpO, Brd], [B_oacc[pg]])
                            else:
                                em.op("dve", lambda e: e.scalar_tensor_tensor(out=oacc[:, pg, :], in0=pO[:, :512], scalar=rd[:, 2:3],
                                                                              in1=oacc[:, pg, :], op0=ALU.mult, op1=ALU.add),
                                      [BpO, Brd, B_oacc[pg]], [B_oacc[pg]])
                nwsrc = (W["gla_out_norm"] if kind == 0 else W["mlstm_out_norm"])[l][h * 512:(h + 1) * 512]
                em.op("sp", lambda e: e.dma_start(out=nwbc[:], in_=nwsrc.partition_broadcast(128)), [], [B_nw], dma=True)
                for pg in range(NPAIR):
                    t, Bt = epf_next()
                    em.op("act", lambda e: e.activation(out=t[:], in_=oacc[:, pg, :], func=AF.Square, accum_out=st4[:, 0:1]),
                          [B_oacc[pg]], [Bt, B_ss])
                    rstd_from_ss(512)
                    em.op("dve", lambda e: e.scalar_tensor_tensor(out=t[:], in0=oacc[:, pg, :], scalar=st4[:, 1:2], in1=nwbc[:],
                                                                  op0=ALU.mult, op1=ALU.mult), [B_oacc[pg], B_rstd, B_nw], [Bt])
                    gi = par["g"] % 2
                    par["g"] += 1
                    em.op("sp", lambda e: e.dma_start(out=gtile[gi][:], in_=gate[pg * 128:(pg + 1) * 128, v0:v0 + 512]),
                          [B_gate[pg][v0 // 512]], [B_gtile[gi]], dma=True)
                    mt, Bm = epb_next()
                    em.op("dve", lambda e: e.tensor_tensor(out=mt[:], in0=t[:], in1=gtile[gi][:], op=ALU.mult), [Bt, B_gtile[gi]], [Bm])
                    em.op("sp", lambda e: e.dma_start(out=merged[pg * 128:(pg + 1) * 128, v0:v0 + 512], in_=mt[:]),
                          [Bm], [B_mg[pg][v0 // 512]], dma=True)
        em.barrier()
        m2.close()
        alloc_core()

        for p in range(NPASS):
            row0 = p * T
            for sub in range(NSUB):
                r0 = row0 + sub * 128
                em.op("sp", lambda e: e.dma_start(out=C.xb[:], in_=merged[r0:r0 + 128, :]), B_mg[r0 // 128], [C.B_xb], dma=True)
                transpose_to_XT(C.xb, C.B_xb, KC, sub, False)
            epi = resid_epilogue(src, Bsrc, xs, B_xs, row0, 1.0)
            linear(C.XT, C.B_XT, KC, NSUB, [([(W["w_out"][l], cb * 512, 512, 0)], epi) for cb in range(8)])

    def xattn(l, src, Bsrc):
        xa = contextlib.ExitStack()

        def sbx(name, shape, dt):
            return xa.enter_context(nc.sbuf_tensor(name + "_%d" % l, list(shape), dt))
        kT = sbx("kT", (128, 4, MEMT), BF16)
        Vb = sbx("Vb", (128, 2, 512), BF16)
        B_kT, B_Vb = Buf("kT"), Buf("Vb")
        qT = sbx("qT", (128, 4, 128), BF16)
        B_qT = Buf("qT")
        pb = sbx("pb", (128, MEMT), BF16)
        B_pb = Buf("pb")
        pT = sbx("pT", (128, 2, 128), BF16)
        B_pT = Buf("pT")
        ob = sbx("ob", (128, 512), BF16)
        B_ob = Buf("ob")
        OT = sbx("OT", (128, 4, T), BF16)
        B_OT = [[Buf("OT%d_%d" % (c, s)) for s in range(NSUB)] for c in range(4)]
        sm = sbx("sm", (128, 4), F32)
        B_sm = Buf("sm")
        scale = 128.0 ** -0.5
        load_wcol(W["mem_norm"][l])
        B_mem = [[Buf("mem", const=True)] for _ in range(2)]
        norm_stage(mem_in, B_mem, 0, 2)

        def epi_k(bi, sub, pst, Bps):
            t, Bt = epb_next()
            copy_op("act", t[:], pst[:, :512], [Bps], [Bt])
            tr, Btr = tr_next()

            def fn(e):
                for hh in range(4):
                    ins = e.transpose(out=tr[:, hh * 128:(hh + 1) * 128], in_=t[:, hh * 128:(hh + 1) * 128], identity=identb[:])
                return ins
            em.op("pe", fn, [Bt, B_cb], [Btr])
            copy_op("dve", kT[:, :, sub * 128:(sub + 1) * 128], tr[:, 0:512].rearrange("p (h t) -> p h t", t=128), [Btr], [B_kT])

        def epi_v(bi, sub, pst, Bps):
            copy_op("act", Vb[:, sub, :], pst[:, :512], [Bps], [B_Vb])
        linear(C.XT, C.B_XT, KC, 2, [([(W["xattn_wk"][l], 0, 512, 0)], epi_k), ([(W["xattn_wv"][l], 0, 512, 0)], epi_v)])
        load_wcol(W["xattn_norm"][l])
        for p in range(NPASS):
            row0 = p * T
            norm_stage(src, Bsrc, row0, NSUB)

            qsave = {}

            def epi_q(bi, sub, pst, Bps):
                t, Bt = epb_next()
                em.op("dve", lambda e: e.tensor_scalar(out=t[:], in0=pst[:, :512], scalar1=scale, scalar2=None, op0=ALU.mult), [Bps], [Bt])
                qsave[sub] = (t, Bt)

            def post_q(bi, sub):
                t, Bt = qsave[sub]
                tr, Btr = tr_next()

                def fn(e):
                    for hh in range(4):
                        ins = e.transpose(out=tr[:, hh * 128:(hh + 1) * 128], in_=t[:, hh * 128:(hh + 1) * 128], identity=identb[:])
                    return ins
                em.op("pe", fn, [Bt, B_cb], [Btr])
                copy_op("dve", qT[:], tr[:, 0:512].rearrange("p (h t) -> p h t", t=128), [Btr], [B_qT])
                for hh in range(4):
                    pS, BpS = acc_next()
                    em.op("pe", lambda e: e.matmul(pS[:, :MEMT], lhsT=qT[:, hh, :], rhs=kT[:, hh, :], start=True, stop=True),
                          [B_qT, B_kT], [BpS])
                    em.op("dve", lambda e: e.tensor_reduce(out=sm[:, 0:1], in_=pS[:, :MEMT], axis=mybir.AxisListType.X, op=ALU.max),
                          [BpS], [B_sm])
                    em.op("dve", lambda e: e.tensor_scalar(out=sm[:, 1:2], in0=sm[:, 0:1], scalar1=-1.0, scalar2=None, op0=ALU.mult),
                          [B_sm], [B_sm])
                    em.op("act", lambda e: e.activation(out=pb[:], in_=pS[:, :MEMT], func=AF.Exp, bias=sm[:, 1:2], accum_out=sm[:, 2:3]),
                          [BpS, B_sm], [B_pb, B_sm])
                    em.op("dve", lambda e: e.reciprocal(out=sm[:, 3:4], in_=sm[:, 2:3]), [B_sm], [B_sm])
                    tr2, Btr2 = tr_next()

                    def fn2(e):
                        for mt in range(2):
                            ins = e.transpose(out=tr2[:, mt * 128:(mt + 1) * 128], in_=pb[:, mt * 128:(mt + 1) * 128], identity=identb[:])
                        return ins
                    em.op("pe", fn2, [B_pb, B_cb], [Btr2])
                    copy_op("act", pT[:], tr2[:, 0:256].rearrange("p (m t) -> p m t", t=128), [Btr2], [B_pT])
                    pO, BpO = acc_next()

                    def fn3(e):
                        for mt in range(2):
                            ins = e.matmul(pO[:, 0:128], lhsT=pT[:, mt, :], rhs=Vb[:, mt, hh * 128:(hh + 1) * 128], start=(mt == 0), stop=(mt == 1))
                        return ins
                    em.op("pe", fn3, [B_pT, B_Vb], [BpO])
                    em.op("dve", lambda e: e.tensor_scalar(out=ob[:, hh * 128:(hh + 1) * 128], in0=pO[:, 0:128], scalar1=sm[:, 3:4], scalar2=None,
                                                           op0=ALU.mult), [BpO, B_sm], [B_ob])
                transpose_to_XT(ob, B_ob, 4, sub, False, xt=OT, bxt=B_OT)
            linear(C.XT, C.B_XT, KC, NSUB, [([(W["xattn_wq"][l], 0, 512, 0)], epi_q, post_q)])
            epi = resid_epilogue(src, Bsrc, xs, B_xs, row0, 1.0)
            linear(OT, B_OT, 4, NSUB, [([(W["xattn_wo"][l], cb * 512, 512, 0)], epi) for cb in range(8)])
        em.barrier()
        xa.close()

    cur, Bcur = x_in, B_xin
    for l in range(DEPTH):
        if "ffn1" in stages:
            ffn(l, "ffn1", cur, Bcur)
            if KDBG in (0, 3):
                cur, Bcur = xs, B_xs
        if "mix" in stages:
            mixer(l, cur, Bcur)
            cur, Bcur = xs, B_xs
        if "xattn" in stages:
            xattn(l, cur, Bcur)
            cur, Bcur = xs, B_xs
        if "ffn2" in stages:
            ffn(l, "ffn2", cur, Bcur)
            cur, Bcur = xs, B_xs
    fin = final_stage(cur, Bcur)
    em.finish()
    fin.close()
    free_core()
    es.close()
    return nc


_WNAMES = ("ffn1_norm", "ffn1_w_gate", "ffn1_w_up", "ffn1_w_down", "mix_norm", "w_in", "gla_w_lr", "gla_b_lr",
           "gla_out_norm", "mlstm_gate_b", "mlstm_out_norm", "w_out", "xattn_norm", "mem_norm", "xattn_wq",
           "xattn_wk", "xattn_wv", "xattn_wo", "ffn2_norm", "ffn2_w_gate", "ffn2_w_up", "ffn2_w_down")


def run(seqs, mems, weights, L, DFF, DEPTH, stages=("ffn1", "mix", "xattn", "ffn2"), n_cores=None):
    nc = build(L, DFF, DEPTH, stages)
    consts = make_consts()
    n_cores = n_cores or len(seqs)
    shared = {k: np.ascontiguousarray(v, dtype=np.float32) for k, v in weights.items()}
    shared["mlstm_gate_b"] = shared["mlstm_gate_b"].reshape(DEPTH, 16)
    in_maps = []
    for c in range(n_cores):
        m = dict(shared)
        m["x"] = np.ascontiguousarray(seqs[c % len(seqs)], dtype=np.float32)
        m["mem"] = np.ascontiguousarray(mems[c % len(mems)], dtype=np.float32)
        m["consts"] = consts
        in_maps.append(m)
    res = run_bass_kernel_spmd(nc, in_maps, core_ids=list(range(n_cores)))
    return [r["y"] for r in res.results]


def kernel(x_prompt, x_sample, mem_prompt, mem_sample, **w):
    seqs = [x_prompt[i] for i in range(x_prompt.shape[0])] + [x_sample[i] for i in range(x_sample.shape[0])]
    mems = [mem_prompt[i] for i in range(mem_prompt.shape[0])] + [mem_sample[i] for i in range(mem_sample.shape[0])]
    weights = {k: w[k] for k in _WNAMES}
    weights["final_norm"] = w["final_norm"]
    L = x_prompt.shape[1]
    ys = run(seqs, mems, weights, L, w["ffn1_w_gate"].shape[2], w["ffn1_w_gate"].shape[0], n_cores=8)
    nb = x_prompt.shape[0]
    y_prompt = np.stack(ys[:nb]).astype(np.float32)
    y_sample = np.stack(ys[nb:nb + x_sample.shape[0]]).astype(np.float32)
    return (y_prompt, y_sample)
```

```python
import contextlib
import os
import numpy as np
KDBG = int(os.environ.get('KDBG', '0'))
import concourse.bass as bass
import concourse.mybir as mybir
from concourse.bass_utils import run_bass_kernel_spmd

F32 = mybir.dt.float32
BF16 = mybir.dt.bfloat16
AF = mybir.ActivationFunctionType
ALU = mybir.AluOpType

D = 4096
KC = D // 128
MEMT = 256
EPS = 1e-6
SEM_CAP = 30000
GQ, GK, GV, GG, GLRF, GLRB, MQ, MK, MV, MO, MG, INW = (
    0, 1024, 2048, 4096, 6144, 6160, 6176, 7200, 8224, 10272, 12320, 12336)

C_ID, C_MF, C_MB, C_MBS, C_MFS, C_IND, C_ONE, C_EPS, C_ZERO, NCONST = 0, 128, 256, 384, 512, 640, 642, 643, 644, 644 + 256


def make_consts():
    i = np.arange(128)[:, None]
    j = np.arange(128)[None, :]
    same = (i // 64) == (j // 64)
    c = np.zeros((128, NCONST), np.float32)
    c[:, C_ID:C_ID + 128] = (i == j)
    c[:, C_MF:C_MF + 128] = same & (i <= j)
    c[:, C_MB:C_MB + 128] = same & (i >= j)
    c[:, C_MBS:C_MBS + 128] = same & (i > j)
    c[:, C_MFS:C_MFS + 128] = same & (i < j)
    c[:, C_IND] = (np.arange(128) < 64)
    c[:, C_IND + 1] = (np.arange(128) >= 64)
    c[:, C_ONE] = 1.0
    c[:, C_EPS] = EPS
    return c


class Buf:
    __slots__ = ("name", "w", "r", "al", "const")

    def __init__(self, name, const=False):
        self.name = name
        self.w = None
        self.r = {}
        self.al = [self]
        self.const = const


class Em:
    COMPUTE = ("pe", "act", "dve", "pool")

    def __init__(self, nc, es):
        self.nc, self.es = nc, es
        self.E = {"pe": nc.tensor, "act": nc.scalar, "dve": nc.vector, "pool": nc.gpsimd, "sp": nc.sync}
        self.nsem = 0
        self.csem = {e: self._newsem() for e in self.COMPUTE}
        self.ccnt = {e: 0 for e in self.COMPUTE}
        self.seq = {e: 0 for e in self.COMPUTE}
        self.seen = {e: {} for e in self.E}
        self.dring = {"sp": [[self._newsem(), 0] for _ in range(16)],
                      "pool": [[self._newsem(), 0] for _ in range(8)]}
        self.dpos = {"sp": 0, "pool": 0}
        self.ndma = 0
        self.oldsems = []

    def _newsem(self):
        self.nsem += 1
        return self.es.enter_context(self.nc.semaphore("s%d" % self.nsem))

    def _wait(self, eng, sig):
        sem, val = sig[0], sig[1]
        seen = self.seen[eng]
        k = id(sem)
        if seen.get(k, 0) >= val:
            return
        self.E[eng].wait_ge(sem, val)
        seen[k] = val

    def op(self, eng, fn, reads=(), writes=(), dma=False):
        deps = []
        for b in reads:
            for a in b.al:
                if a.w is not None:
                    deps.append(a.w)
        for b in writes:
            for a in b.al:
                if a.w is not None:
                    deps.append(a.w)
                if a.r:
                    deps.extend(a.r.values())
        for sig in deps:
            if (not dma) and sig[2] == eng:
                if eng == "pe":
                    continue
                if self.seq[eng] - sig[3] >= 6:
                    continue
            self._wait(eng, sig)
        if dma:
            ring = self.dring[eng]
            slot = ring[self.dpos[eng] % len(ring)]
            self.dpos[eng] += 1
            if slot[1] > 0:
                self._wait(eng, (slot[0], slot[1]))
            if slot[1] + 16 > SEM_CAP:
                self.oldsems.append((slot[0], slot[1]))
                slot[0] = self._newsem()
                slot[1] = 0
            ins = fn(self.E[eng])
            slot[1] += 16
            ins.then_inc(slot[0], 16)
            sig = (slot[0], slot[1], "dma", 0)
            self.ndma += 1
            key = ("d", self.ndma)
        else:
            if self.ccnt[eng] >= SEM_CAP:
                self.csem[eng] = self._newsem()
                self.ccnt[eng] = 0
            ins = fn(self.E[eng])
            self.ccnt[eng] += 1
            self.seq[eng] += 1
            ins.then_inc(self.csem[eng], 1)
            sig = (self.csem[eng], self.ccnt[eng], eng, self.seq[eng])
            key = eng
        for b in reads:
            if not b.const:
                b.r[key] = sig
        for b in writes:
            for a in b.al:
                a.w = sig
                a.r = {}
        return ins

    def barrier(self):
        sigs = [(self.csem[e], self.ccnt[e]) for e in self.COMPUTE if self.ccnt[e] > 0]
        for q in self.dring:
            sigs += [(s[0], s[1]) for s in self.dring[q] if s[1] > 0]
        for e in self.E:
            for sg in sigs:
                self._wait(e, sg)

    def finish(self):
        for q in self.dring:
            for s in self.dring[q]:
                if s[1] > 0:
                    self._wait("sp", (s[0], s[1]))


def build(L, DFF, DEPTH, stages=("ffn1", "mix", "xattn", "ffn2")):
    assert L % 128 == 0 and DFF % 256 == 0
    T = min(512, L)
    NSUB = T // 128
    NPASS = L // T
    FC = DFF // 128
    NPAIR = L // 128
    NCH = L // 64
    PB = min(8, NPAIR)

    nc = bass.Bass("TRN2", target_bir_lowering=False)
    es = contextlib.ExitStack()
    em = Em(nc, es)

    def din(name, shape):
        return nc.dram_tensor(name, list(shape), F32, kind="ExternalInput").ap()

    def dint(name, shape, dt):
        return nc.dram_tensor(name, list(shape), dt, kind="Internal").ap()

    x_in = din("x", (L, D))
    mem_in = din("mem", (MEMT, D))
    consts_in = din("consts", (128, NCONST))
    W = {}
    for nm, shp in (("ffn1_norm", (D,)), ("ffn1_w_gate", (D, DFF)), ("ffn1_w_up", (D, DFF)),
                    ("ffn1_w_down", (DFF, D)), ("mix_norm", (D,)), ("w_in", (D, INW)),
                    ("gla_w_lr", (2, 16, 1024)), ("gla_b_lr", (2, 1024)), ("gla_out_norm", (2048,)),
                    ("mlstm_gate_b", (16,)), ("mlstm_out_norm", (2048,)), ("w_out", (D, D)),
                    ("xattn_norm", (D,)), ("mem_norm", (D,)), ("xattn_wq", (D, 512)),
                    ("xattn_wk", (D, 512)), ("xattn_wv", (D, 512)), ("xattn_wo", (512, D)),
                    ("ffn2_norm", (D,)), ("ffn2_w_gate", (D, DFF)), ("ffn2_w_up", (D, DFF)),
                    ("ffn2_w_down", (DFF, D))):
        W[nm] = din(nm, (DEPTH,) + shp)
    final_norm = din("final_norm", (D,))
    y_out = nc.dram_tensor("y", [L, D], F32, kind="ExternalOutput").ap()

    xs = dint("xs", (L, D), F32)
    qd = [dint("qd%d" % i, (L, 2048), BF16) for i in range(2)]
    ki = [dint("ki%d" % i, (L, 2048), BF16) for i in range(2)]
    ke = [dint("ke%d" % i, (L, 2048), BF16) for i in range(2)]
    vv = dint("vv", (L, 4096), BF16)
    gate = dint("gate", (L, 4096), BF16)
    merged = dint("merged", (L, 4096), BF16)

    def dbufs(name, ncb):
        return [[Buf("%s_%d_%d" % (name, r, c)) for c in range(ncb)] for r in range(NPAIR)]
    B_xs = dbufs("xs", 8)
    B_xin = [[Buf("xin", const=True) for c in range(8)] for r in range(NPAIR)]
    B_y = dbufs("y", 8)
    B_qd = [dbufs("qd%d" % i, 4) for i in range(2)]
    B_ki = [dbufs("ki%d" % i, 4) for i in range(2)]
    B_ke = [dbufs("ke%d" % i, 4) for i in range(2)]
    B_vv = dbufs("vv", 8)
    B_gate = dbufs("gate", 8)
    B_mg = dbufs("mg", 8)
    B_w = Buf("weights", const=True)

    def sb(name, shape, dt):
        return es.enter_context(nc.sbuf_tensor(name, list(shape), dt))

    def ps(name, shape, dt):
        return es.enter_context(nc.psum_tensor(name, list(shape), dt))

    cst = sb("cst", (128, NCONST), F32)
    B_cst = Buf("cst", const=True)
    identb = sb("identb", (128, 128), BF16)
    onesb = sb("onesb", (128, 1), BF16)
    B_cb = Buf("cb", const=True)
    import types
    C = types.SimpleNamespace()
    core_es = [None]
    gen = [0]

    def alloc_core():
        gen[0] += 1
        ce = contextlib.ExitStack()
        core_es[0] = ce

        def sbc(name, shape, dt):
            return ce.enter_context(nc.sbuf_tensor("%s_g%d" % (name, gen[0]), list(shape), dt))
        C.XT = sbc("XT", (128, KC, T), BF16)
        C.B_XT = [[Buf("XT%d_%d" % (c, s)) for s in range(NSUB)] for c in range(KC)]
        C.wslot = [sbc("wslot%d" % i, (128, 4096), BF16) for i in range(NSLOT)]
        C.B_ws = [Buf("ws%d" % i) for i in range(NSLOT)]
        C.xtile = sbc("xtile", (128, D), F32)
        C.B_xtile = Buf("xtile")
        C.xb = sbc("xb", (128, D), BF16)
        C.B_xb = Buf("xb")

    def free_core():
        core_es[0].close()
    NSLOT = 4
    st4 = sb("st4", (128, 8), F32)
    B_ss, B_rstd = Buf("ss"), Buf("rstd")
    wcol = sb("wcol", (128, KC), F32)
    wrow = sb("wrow", (KC, 128), F32)
    B_wrow = Buf("wrow")
    B_wcol = Buf("wcol")
    accb = [ps("acc%d" % i, (128, 512), F32) for i in range(6)]
    B_acc = [Buf("acc%d" % i) for i in range(6)]
    trp = [ps("trp%d" % i, (128, 1024), BF16) for i in range(2)]
    B_tr = [Buf("tr%d" % i) for i in range(4)]
    state = {"acc": 0, "tr": 0, "ws": 0, "alt": 0}

    def acc_next():
        i = state["acc"] % 6
        state["acc"] += 1
        return accb[i], B_acc[i]

    def tr_next():
        i = state["tr"] % 2
        state["tr"] += 1
        return trp[i][:, 0:512], B_tr[i]

    def alt():
        state["alt"] += 1
        return "act" if state["alt"] % 2 else "dve"

    def copy_op(eng, out, in_, reads, writes):
        if eng == "act":
            em.op("act", lambda e: e.activation(out=out, in_=in_, func=AF.Copy), reads, writes)
        else:
            em.op("dve", lambda e: e.tensor_copy(out=out, in_=in_), reads, writes)

    em.op("sp", lambda e: e.dma_start(out=cst[:], in_=consts_in), [], [B_cst], dma=True)
    em.op("dve", lambda e: e.tensor_copy(out=identb[:], in_=cst[:, C_ID:C_ID + 128]), [B_cst], [B_cb])
    em.op("dve", lambda e: e.tensor_copy(out=onesb[:], in_=cst[:, C_ONE:C_ONE + 1]), [B_cst], [B_cb])
    B_cst.const = True
    one_col = cst[:, C_ONE:C_ONE + 1]
    eps_col = cst[:, C_EPS:C_EPS + 1]

    def load_wcol(w1d):
        em.op("sp", lambda e: e.dma_start(out=wrow[:], in_=w1d.rearrange("(c p) -> c p", p=128)), [], [B_wrow], dma=True)
        pst, Bps = acc_next()
        em.op("pe", lambda e: e.transpose(out=pst[:, 0:KC], in_=wrow[:], identity=cst[0:KC, C_ID:C_ID + KC]), [B_wrow, B_cst], [Bps])
        em.op("dve", lambda e: e.tensor_copy(out=wcol[:], in_=pst[:, 0:KC]), [Bps], [B_wcol])

    def rstd_from_ss(n):
        em.op("act", lambda e: e.activation(out=st4[:, 1:2], in_=st4[:, 0:1], func=AF.Sqrt, scale=1.0 / n, bias=eps_col),
              [B_ss, B_cst], [B_rstd])
        em.op("dve", lambda e: e.reciprocal(out=st4[:, 1:2], in_=st4[:, 1:2]), [B_rstd], [B_rstd])

    def transpose_to_XT(src_bf, B_src, nchunk, sub, scale_cols, xt=None, bxt=None, tok=128):
        xt = C.XT if xt is None else xt
        bxt = C.B_XT if bxt is None else bxt
        for c4 in range(0, nchunk, 4):
            n = min(4, nchunk - c4)
            tr, Btr = tr_next()

            def fn(e):
                for j in range(n):
                    ins = e.transpose(out=tr[:, j * 128:(j + 1) * 128],
                                      in_=src_bf[:, (c4 + j) * 128:(c4 + j + 1) * 128], identity=identb[:])
                return ins
            em.op("pe", fn, [B_src, B_cb], [Btr])
            if scale_cols:
                for j in range(n):
                    c = c4 + j
                    eng = "dve"
                    o = xt[:, c, sub * 128:(sub + 1) * 128]
                    i_ = tr[:, j * 128:(j + 1) * 128]
                    if eng == "act":
                        em.op("act", lambda e: e.activation(out=o, in_=i_, func=AF.Copy, scale=wcol[:, c:c + 1]),
                              [Btr, B_wcol], [bxt[c][sub]])
                    else:
                        em.op("dve", lambda e: e.tensor_scalar(out=o, in0=i_, scalar1=wcol[:, c:c + 1], scalar2=None,
                                                               op0=ALU.mult), [Btr, B_wcol], [bxt[c][sub]])
            else:
                o = xt[:, c4:c4 + n, sub * 128:(sub + 1) * 128]
                i_ = tr[:, 0:n * 128].rearrange("p (c t) -> p c t", t=128)
                copy_op(alt(), o, i_, [Btr], [bxt[c4 + j][sub] for j in range(n)])

    def norm_s1(src, Bsrc, r0):
        em.op("sp", lambda e: e.dma_start(out=C.xtile[:], in_=src[r0:r0 + 128, :]), Bsrc[r0 // 128], [C.B_xtile], dma=True)
        em.op("act", lambda e: e.activation(out=C.xb[:], in_=C.xtile[:], func=AF.Square, accum_out=st4[:, 0:1]),
              [C.B_xtile], [C.B_xb, B_ss])
        rstd_from_ss(D)
        em.op("dve", lambda e: e.tensor_scalar(out=C.xb[:], in0=C.xtile[:], scalar1=st4[:, 1:2], scalar2=None,
                                               op0=ALU.mult), [C.B_xtile, B_rstd], [C.B_xb])

    def norm_s2(sub):
        transpose_to_XT(C.xb, C.B_xb, KC, sub, True)

    def norm_stage(src, Bsrc, row0, nsub):
        for sub in range(nsub):
            norm_s1(src, Bsrc, row0 + sub * 128)
            norm_s2(sub)

    def norm_hooks(src, Bsrc, row0, nsub):
        hooks = {}
        for sub in range(nsub + 1):
            def hk(sub=sub):
                if sub > 0:
                    norm_s2(sub - 1)
                if sub < nsub:
                    norm_s1(src, Bsrc, row0 + sub * 128)
            hooks[1 + sub] = hk
        return hooks

    class WStream:
        def __init__(self):
            self.pending = []
            self.issued = []

        def plan(self, tiles):
            self.pending.extend(tiles)

        def _issue(self):
            wap, k0, kn, c0, cw = self.pending.pop(0)
            i = state["ws"] % NSLOT
            state["ws"] += 1
            view = C.wslot[i][:, :kn * cw].rearrange("p (k c) -> p k c", c=cw)
            srcap = wap[k0 * 128:(k0 + kn) * 128, c0:c0 + cw].rearrange("(k p) c -> p k c", p=128)
            em.op("pool", lambda e: e.dma_start(out=view, in_=srcap), [B_w], [C.B_ws[i]], dma=True)
            self.issued.append((view, C.B_ws[i]))

        def get(self):
            while self.pending and len(self.issued) < NSLOT - 1:
                self._issue()
            if not self.issued:
                self._issue()
            r = self.issued.pop(0)
            while self.pending and len(self.issued) < NSLOT - 1:
                self._issue()
            return r

    ws = WStream()

    def linear(xt, bxt, K, nsub, blocks, hooks=None):
        kper = {}
        for blk_ in blocks:
            parts = blk_[0]
            for (wap, c0, cw, bc) in parts:
                kp = max(1, min(K, 4096 // cw))
                for k0 in range(0, K, kp):
                    ws.plan([(wap, k0, min(kp, K - k0), c0, cw)])
        deferred = []
        for bi, blk_ in enumerate(blocks):
            parts, epi = blk_[0], blk_[1]
            post = blk_[2] if len(blk_) > 2 else None
            banks = [acc_next() for _ in range(nsub)]
            first_tile = True
            for (wap, c0, cw, bc) in parts:
                kp = max(1, min(K, 4096 // cw))
                for k0 in range(0, K, kp):
                    kn = min(kp, K - k0)
                    wt, Bw = ws.get()
                    if KDBG == 2:
                        continue
                    for sub in range(nsub):
                        def fn(e):
                            for k in range(kn):
                                ins = e.matmul(banks[sub][0][:, bc:bc + cw], lhsT=xt[:, k0 + k, sub * 128:(sub + 1) * 128],
                                               rhs=wt[:, k, :], start=(k0 + k == 0), stop=(k0 + k == K - 1))
                            return ins
                        em.op("pe", fn, [Bw] + [bxt[k0 + k][sub] for k in range(kn)], [banks[sub][1]])
                    if first_tile:
                        first_tile = False
                        for f2 in deferred:
                            f2()
                        deferred = []
                        if hooks and bi in hooks:
                            hooks[bi]()
            for sub in range(nsub):
                if KDBG == 2 or KDBG == 3:
                    continue
                r2 = epi(bi, sub, banks[sub][0], banks[sub][1])
                if r2 is not None:
                    deferred.append(r2)
            if post is not None:
                for sub in range(nsub):
                    post(bi, sub)
        for f2 in deferred:
            f2()

    epf = [sb("epf%d" % i, (128, 512), F32) for i in range(3)]
    B_epf = [Buf("epf%d" % i) for i in range(3)]
    epb = [sb("epb%d" % i, (128, 512), BF16) for i in range(6)]
    B_epb = [Buf("epb%d" % i) for i in range(6)]
    rot = {"f": 0, "b": 0}

    dec = sb("dec", (128, 4 * 8, NCH), F32)
    B_dec = Buf("dec")
    alloc_core()

    def epf_next():
        i = rot["f"] % 3
        rot["f"] += 1
        return epf[i], B_epf[i]

    def epb_next():
        i = rot["b"] % 6
        rot["b"] += 1
        return epb[i], B_epb[i]

    def resid_epilogue(src, Bsrc, dst, Bdst, row0, scale):
        def epi(cb, sub, pst, Bps):
            r0 = row0 + sub * 128
            t, Bt = epf_next()
            em.op("sp", lambda e: e.dma_start(out=t[:], in_=src[r0:r0 + 128, cb * 512:(cb + 1) * 512]),
                  [Bsrc[r0 // 128][cb]], [Bt], dma=True)
            em.op("dve", lambda e: e.scalar_tensor_tensor(out=t[:], in0=pst[:, :512], scalar=scale, in1=t[:],
                                                          op0=ALU.mult, op1=ALU.add), [Bps, Bt], [Bt])
            em.op("sp", lambda e: e.dma_start(out=dst[r0:r0 + 128, cb * 512:(cb + 1) * 512], in_=t[:]),
                  [Bt], [Bdst[r0 // 128][cb]], dma=True)
        return epi

    def ffn(l, pre, src, Bsrc):
        fst = contextlib.ExitStack()
        HT = fst.enter_context(nc.sbuf_tensor("HT_%s%d" % (pre, l), [128, FC, T], BF16))
        B_HT = [[Buf("HT%d_%d" % (c, s)) for s in range(NSUB)] for c in range(FC)]
        load_wcol(W[pre + "_norm"][l])
        wg, wu, wd = W[pre + "_w_gate"][l], W[pre + "_w_up"][l], W[pre + "_w_down"][l]
        for p in range(NPASS):
            row0 = p * T
            if p == 0:
                norm_stage(src, Bsrc, row0, NSUB)

            def epiA(cb, sub, pst, Bps):
                t, Bt = epf_next()
                em.op("act", lambda e: e.activation(out=t[:, :256], in_=pst[:, 0:256], func=AF.Silu), [Bps], [Bt])
                hb, Bh = epb_next()
                em.op("dve", lambda e: e.tensor_tensor(out=hb[:, :256], in0=t[:, :256], in1=pst[:, 256:512], op=ALU.mult),
                      [Bt, Bps], [Bh])
                return lambda: transpose_to_XT(hb, Bh, 2, sub, False, xt=HT[:, 2 * cb:2 * cb + 2, :], bxt=B_HT[2 * cb:2 * cb + 2])
            blocksA = [([(wg, cb * 256, 256, 0), (wu, cb * 256, 256, 256)], epiA) for cb in range(DFF // 256)]
            linear(C.XT, C.B_XT, KC, NSUB, blocksA)
            epiB = resid_epilogue(src, Bsrc, xs, B_xs, row0, 0.5)
            blocksB = [([(wd, cb * 512, 512, 0)], epiB) for cb in range(8)]
            hooks = norm_hooks(src, Bsrc, row0 + T, NSUB) if p + 1 < NPASS else None
            linear(HT, B_HT, FC, NSUB, blocksB, hooks=hooks)
        em.barrier()
        fst.close()

    def final_stage(src, Bsrc):
        fin = contextlib.ExitStack()
        wbc = fin.enter_context(nc.sbuf_tensor("fn_wbc", [128, D], F32))
        B_wbc = Buf("wbc")
        em.op("sp", lambda e: e.dma_start(out=wbc[:], in_=final_norm.partition_broadcast(128)), [], [B_wbc], dma=True)
        for r in range(NPAIR):
            em.op("sp", lambda e: e.dma_start(out=C.xtile[:], in_=src[r * 128:(r + 1) * 128, :]), Bsrc[r], [C.B_xtile], dma=True)
            em.op("act", lambda e: e.activation(out=C.xb[:], in_=C.xtile[:], func=AF.Square, accum_out=st4[:, 0:1]),
                  [C.B_xtile], [C.B_xb, B_ss])
            rstd_from_ss(D)
            em.op("dve", lambda e: e.scalar_tensor_tensor(out=C.xtile[:], in0=C.xtile[:], scalar=st4[:, 1:2], in1=wbc[:],
                                                          op0=ALU.mult, op1=ALU.mult), [C.B_xtile, B_rstd, B_wbc], [C.B_xtile])
            em.op("sp", lambda e: e.dma_start(out=y_out[r * 128:(r + 1) * 128, :], in_=C.xtile[:]), [C.B_xtile], B_y[r], dma=True)
        return fin

    def mixer(l, src, Bsrc):
        w_in = W["w_in"][l]
        m1 = contextlib.ExitStack()

        def sbm(name, shape, dt):
            return m1.enter_context(nc.sbuf_tensor(name + "_%d" % l, list(shape), dt))
        fac = sbm("fac", (128, NSUB * 6, 1024), BF16)
        B_fac = [[Buf("fac%d_%d" % (s, d)) for d in range(2)] for s in range(NSUB)]
        wsm = sbm("wsm", (128, KC, 48), BF16)
        B_wsm = Buf("wsm")
        wlr = sbm("wlr", (16, 2, 1024), BF16)
        blr = sbm("blr", (128, 2, 1024), F32)
        gbias = sbm("gbias", (128, 16), F32)
        B_misc = Buf("m1misc")
        lrT = sbm("lrT", (16, 2, T), BF16)
        B_lrT = Buf("lrT")
        gts = sbm("gts", (128, NSUB, 16), F32)
        lsg = sbm("lsg", (128, NSUB, 16), F32)
        eig = sbm("eig", (128, NSUB, 16), F32)
        B_gts = [Buf("gts%d" % s) for s in range(NSUB)]
        gt = [sbm("gt%d" % i, (128, 1024), F32) for i in range(2)]
        B_gt = [Buf("gt%d" % i) for i in range(2)]
        ztmp = [sbm("zt%d" % i, (128, 512), F32) for i in range(2)]
        B_zt = [Buf("zt%d" % i) for i in range(2)]
        rr = {"gt": 0, "zt": 0}

        em.op("pool", lambda e: e.dma_start(out=wsm[:, :, 0:32], in_=w_in[:, GLRF:GLRF + 32].rearrange("(k p) c -> p k c", p=128)),
              [], [B_wsm], dma=True)
        em.op("pool", lambda e: e.dma_start(out=wsm[:, :, 32:48], in_=w_in[:, MG:MG + 16].rearrange("(k p) c -> p k c", p=128)),
              [], [B_wsm], dma=True)
        em.op("pool", lambda e: e.dma_start(out=wlr[:], in_=W["gla_w_lr"][l].rearrange("d r c -> r d c")), [], [B_misc], dma=True)
        em.op("sp", lambda e: e.dma_start(out=blr[:].rearrange("p d c -> p (d c)"),
                                          in_=W["gla_b_lr"][l].rearrange("d c -> (d c)").partition_broadcast(128)),
              [], [B_misc], dma=True)
        em.op("sp", lambda e: e.dma_start(out=gbias[:], in_=W["mlstm_gate_b"][l].partition_broadcast(128)), [], [B_misc], dma=True)
        load_wcol(W["mix_norm"][l])

        for p in range(NPASS):
            row0 = p * T
            norm_stage(src, Bsrc, row0, NSUB)
            for d_ in range(2):
                pst, Bps = acc_next()

                def fn(e):
                    for k in range(KC):
                        ins = e.matmul(pst[0:16, :T], lhsT=wsm[:, k, d_ * 16:(d_ + 1) * 16], rhs=C.XT[:, k, :],
                                       start=(k == 0), stop=(k == KC - 1))
                    return ins
                em.op("pe", fn, [B_wsm] + [C.B_XT[k][s] for k in range(KC) for s in range(NSUB)], [Bps])
                copy_op("act", lrT[:, d_, :], pst[0:16, :T], [Bps], [B_lrT])
            for sub in range(NSUB):
                pst, Bps = acc_next()

                def fn(e):
                    for k in range(KC):
                        ins = e.matmul(pst[:, 0:16], lhsT=C.XT[:, k, sub * 128:(sub + 1) * 128], rhs=wsm[:, k, 32:48],
                                       start=(k == 0), stop=(k == KC - 1))
                    return ins
                em.op("pe", fn, [B_wsm] + [C.B_XT[k][sub] for k in range(KC)], [Bps])
                em.op("dve", lambda e: e.tensor_tensor(out=gts[:, sub, :], in0=pst[:, 0:16], in1=gbias[:], op=ALU.add),
                      [Bps, B_misc], [B_gts[sub]])
                em.op("act", lambda e: e.activation(out=eig[:, sub, :], in_=gts[:, sub, :], func=AF.Exp), [B_gts[sub]], [B_gts[sub]])
                em.op("act", lambda e: e.activation(out=lsg[:, sub, :], in_=gts[:, sub, :], func=AF.Exp, scale=-1.0),
                      [B_gts[sub]], [B_gts[sub]])
                em.op("act", lambda e: e.activation(out=lsg[:, sub, :], in_=lsg[:, sub, :], func=AF.Ln, bias=one_col),
                      [B_gts[sub]], [B_gts[sub]])
                em.op("dve", lambda e: e.tensor_scalar(out=lsg[:, sub, :], in0=lsg[:, sub, :], scalar1=-1.0, scalar2=None,
                                                       op0=ALU.mult), [B_gts[sub]], [B_gts[sub]])

            def factors(kind):
                for sub in range(NSUB):
                    for d_ in range(2):
                        g = gt[rr["gt"] % 2]
                        Bg = B_gt[rr["gt"] % 2]
                        rr["gt"] += 1
                        if kind == 0:
                            for half in range(2):
                                pst, Bps = acc_next()
                                em.op("pe", lambda e: e.matmul(pst[:, :512], lhsT=lrT[:, d_, sub * 128:(sub + 1) * 128],
                                                               rhs=wlr[:, d_, half * 512:(half + 1) * 512], start=True, stop=True),
                                      [B_lrT, B_misc], [Bps])
                                z = ztmp[rr["zt"] % 2]
                                Bz = B_zt[rr["zt"] % 2]
                                rr["zt"] += 1
                                em.op("dve", lambda e: e.tensor_tensor(out=z[:], in0=pst[:, :512], in1=blr[:, d_, half * 512:(half + 1) * 512],
                                                                       op=ALU.add), [Bps, B_misc], [Bz])
                                em.op("act", lambda e: e.activation(out=z[:], in_=z[:], func=AF.Exp, scale=-1.0), [Bz], [Bz])
                                em.op("act", lambda e: e.activation(out=z[:], in_=z[:], func=AF.Ln, bias=one_col), [Bz], [Bz])
                                em.op("dve", lambda e: e.tensor_scalar(out=g[:, half * 512:(half + 1) * 512], in0=z[:], scalar1=-1.0 / 16.0,
                                                                       scalar2=-1.0, op0=ALU.mult, op1=ALU.max), [Bz], [Bg])
                        else:
                            for h in range(4):
                                col = 4 + 8 * d_ + h
                                em.op("dve", lambda e: e.tensor_scalar(out=g[:, h * 256:(h + 1) * 256], in0=cst[:, C_ZERO:C_ZERO + 256],
                                                                       scalar1=lsg[:, sub, col:col + 1], scalar2=None, op0=ALU.add),
                                      [B_gts[sub], B_cst], [Bg])
                        triP = cst[:, C_MF:C_MF + 128] if d_ == 0 else cst[:, C_MB:C_MB + 128]
                        triE = cst[:, C_MBS:C_MBS + 128] if d_ == 0 else cst[:, C_MFS:C_MFS + 128]
                        fb = sub * 6 + d_ * 3
                        for half in range(2):
                            hs = slice(half * 512, (half + 1) * 512)
                            pP, BpP = acc_next()
                            em.op("pe", lambda e: e.matmul(pP[:, :512], lhsT=triP, rhs=g[:, hs], start=True, stop=True), [Bg, B_cst], [BpP])
                            em.op("act", lambda e: e.activation(out=fac[:, fb + 0, hs], in_=pP[:, :512], func=AF.Exp), [BpP], [B_fac[sub][d_]])
                            em.op("act", lambda e: e.activation(out=fac[:, fb + 1, hs], in_=pP[:, :512], func=AF.Exp, scale=-1.0),
                                  [BpP], [B_fac[sub][d_]])
                            pE, BpE = acc_next()
                            em.op("pe", lambda e: e.matmul(pE[:, :512], lhsT=triE, rhs=g[:, hs], start=True, stop=True), [Bg, B_cst], [BpE])
                            em.op("act", lambda e: e.activation(out=fac[:, fb + 2, hs], in_=pE[:, :512], func=AF.Exp), [BpE], [B_fac[sub][d_]])
                        pB, BpB = acc_next()

                        def fnb(e):
                            for dc in range(8):
                                ins = e.matmul(pB[:, dc * 2:dc * 2 + 2], lhsT=g[:, dc * 128:(dc + 1) * 128], rhs=cst[:, C_IND:C_IND + 2],
                                               start=True, stop=True)
                            return ins
                        em.op("pe", fnb, [Bg, B_cst], [BpB])
                        ci = (row0 + sub * 128) // 64
                        base = (kind * 2 + d_) * 8
                        em.op("act", lambda e: e.activation(out=dec[:, base:base + 8, ci:ci + 2],
                                                            in_=pB[:, 0:16].rearrange("p (a b) -> p a b", b=2), func=AF.Exp),
                              [BpB], [B_dec])

            def store(t, Bt, dst, Bdst, r0, c0):
                em.op("sp", lambda e: e.dma_start(out=dst[r0:r0 + 128, c0:c0 + 512], in_=t[:]), [Bt], [Bdst[r0 // 128][c0 // 512]], dma=True)

            def epi_q(kind, qb):
                def epi(bi, sub, pst, Bps):
                    r0 = row0 + sub * 128
                    for d_ in range(2):
                        t, Bt = epb_next()
                        em.op("dve", lambda e: e.scalar_tensor_tensor(out=t[:], in0=pst[:, :512], scalar=1.0 / 16.0,
                                                                      in1=fac[:, sub * 6 + d_ * 3 + 0, qb * 512:(qb + 1) * 512],
                                                                      op0=ALU.mult, op1=ALU.mult), [Bps, B_fac[sub][d_]], [Bt])
                        store(t, Bt, qd[d_], B_qd[d_], r0, kind * 1024 + qb * 512)
                return epi

            def epi_k(kind, kb):
                def epi(bi, sub, pst, Bps):
                    r0 = row0 + sub * 128
                    for d_ in range(2):
                        for which, (dst, Bdst) in ((1, (ki[d_], B_ki[d_])), (2, (ke[d_], B_ke[d_]))):
                            t, Bt = epb_next()
                            for hh in range(2):
                                h = kb * 2 + hh
                                cs = slice(hh * 256, (hh + 1) * 256)
                                fsl = fac[:, sub * 6 + d_ * 3 + which, kb * 512 + hh * 256: kb * 512 + (hh + 1) * 256]
                                if kind == 0:
                                    em.op("dve", lambda e: e.tensor_tensor(out=t[:, cs], in0=pst[:, cs], in1=fsl, op=ALU.mult),
                                          [Bps, B_fac[sub][d_]], [Bt])
                                else:
                                    col = 8 * d_ + h
                                    em.op("dve", lambda e: e.scalar_tensor_tensor(out=t[:, cs], in0=pst[:, cs], scalar=eig[:, sub, col:col + 1],
                                                                                  in1=fsl, op0=ALU.mult, op1=ALU.mult),
                                          [Bps, B_fac[sub][d_], B_gts[sub]], [Bt])
                            store(t, Bt, dst, Bdst, r0, kind * 1024 + kb * 512)
                return epi

            def epi_act(func, dst, Bdst, c0):
                def epi(bi, sub, pst, Bps):
                    r0 = row0 + sub * 128
                    t, Bt = epb_next()
                    em.op("act", lambda e: e.activation(out=t[:], in_=pst[:, :512], func=func), [Bps], [Bt])
                    store(t, Bt, dst, Bdst, r0, c0)
                return epi

            for kind, (cq, ck, cv, cg, gfunc) in enumerate(((GQ, GK, GV, GG, AF.Silu), (MQ, MK, MV, MO, AF.Sigmoid))):
                factors(kind)
                blocks = []
                for b in range(2):
                    blocks.append(([(w_in, cq + b * 512, 512, 0)], epi_q(kind, b)))
                for b in range(2):
                    blocks.append(([(w_in, ck + b * 512, 512, 0)], epi_k(kind, b)))
                for b in range(4):
                    blocks.append(([(w_in, cv + b * 512, 512, 0)], epi_act(AF.Copy, vv, B_vv, kind * 2048 + b * 512)))
                for b in range(4):
                    blocks.append(([(w_in, cg + b * 512, 512, 0)], epi_act(gfunc, gate, B_gate, kind * 2048 + b * 512)))
                linear(C.XT, C.B_XT, KC, NSUB, blocks)
        em.barrier()
        m1.close()
        free_core()

        m2 = contextlib.ExitStack()

        def sb2(name, shape, dt):
            return m2.enter_context(nc.sbuf_tensor(name + "_%d" % l, list(shape), dt))
        oacc = sb2("oacc", (128, NPAIR, 512), F32)
        B_oacc = [Buf("oacc%d" % i) for i in range(NPAIR)]
        QD = [[sb2("QD%d_%d" % (d, i), (128, PB, 256), BF16) for i in range(2)] for d in range(2)]
        KI = [[sb2("KI%d_%d" % (d, i), (128, PB, 256), BF16) for i in range(2)] for d in range(2)]
        KE = [[sb2("KE%d_%d" % (d, i), (128, PB, 256), BF16) for i in range(2)] for d in range(2)]
        VV = [[sb2("VV%d_%d" % (d, i), (128, PB, 512), BF16) for i in range(2)] for d in range(2)]
        B_blk = [[Buf("blk%d_%d" % (d, i)) for i in range(2)] for d in range(2)]
        S32 = [sb2("S32_%d" % d, (128, 2, 512), F32) for d in range(2)]
        Sb = [sb2("Sb_%d" % d, (128, 2, 512), BF16) for d in range(2)]
        n32 = [sb2("n32_%d" % d, (128, 2), F32) for d in range(2)]
        nb = [sb2("nb_%d" % d, (128, 2), BF16) for d in range(2)]
        B_S = [[Buf("S32_%d_%d" % (d, dc)) for dc in range(2)] for d in range(2)]
        B_Sb = [[Buf("Sb_%d_%d" % (d, dc)) for dc in range(2)] for d in range(2)]
        B_n = [Buf("n32_%d" % d) for d in range(2)]
        B_nb = [Buf("nb_%d" % d) for d in range(2)]
        ATb = [[sb2("ATb%d_%d" % (d, i), (128, 128), BF16) for i in range(2)] for d in range(2)]
        B_AT = [[Buf("AT%d_%d" % (d, i)) for i in range(2)] for d in range(2)]
        qkT = [[sb2("qkT%d_%d" % (d, i), (128, 512), BF16) for i in range(2)] for d in range(2)]
        B_qkT = [[Buf("qkT%d_%d" % (d, i)) for i in range(2)] for d in range(2)]
        rden = [sb2("rden%d" % d, (128, 4), F32) for d in range(2)]
        B_rden = [Buf("rden%d" % d) for d in range(2)]
        nwbc = sb2("nwbc", (128, 512), F32)
        B_nw = Buf("nwbc")
        gtile = [sb2("gtile%d" % i, (128, 512), BF16) for i in range(2)]
        B_gtile = [Buf("gtile%d" % i) for i in range(2)]
        par = {"g": 0, "s": 0}
        NBLK = NPAIR // PB

        for kind in range(2):
            for h in range(4):
                ch0 = kind * 1024 + h * 256
                v0 = kind * 2048 + h * 512
                for d_ in range(2):
                    em.op("dve", lambda e: e.memset(S32[d_][:], 0.0), [], B_S[d_])
                    em.op("dve", lambda e: e.memset(Sb[d_][:], 0.0), [], B_Sb[d_])
                    if kind == 1:
                        em.op("dve", lambda e: e.memset(n32[d_][:], 0.0), [], [B_n[d_]])
                        em.op("dve", lambda e: e.memset(nb[d_][:], 0.0), [], [B_nb[d_]])

                def load_block(d_, blk, bi):
                    rows = slice(blk * PB * 128, (blk + 1) * PB * 128)
                    rt = range(blk * PB, (blk + 1) * PB)
                    for (dstt, srcd, Bd, c0, cw) in ((QD[d_][bi], qd[d_], B_qd[d_], ch0, 256), (KI[d_][bi], ki[d_], B_ki[d_], ch0, 256),
                                                     (KE[d_][bi], ke[d_], B_ke[d_], ch0, 256), (VV[d_][bi], vv, B_vv, v0, 512)):
                        em.op("sp", lambda e: e.dma_start(out=dstt[:], in_=srcd[rows, c0:c0 + cw].rearrange("(r p) c -> p r c", p=128)),
                              [Bd[r][c0 // 512] for r in rt], [B_blk[d_][bi]], dma=True)

                def pair_body(d_, step, first_visit):
                        pg = step if d_ == 0 else NPAIR - 1 - step
                        blk = pg // PB
                        prl = pg % PB
                        bseq = step // PB
                        bi = bseq % 2
                        pi = step % 2
                        Bblk = B_blk[d_][bi]
                        QDt, KIt, KEt, VVt = QD[d_][bi], KI[d_][bi], KE[d_][bi], VV[d_][bi]
                        qk, Bqk = qkT[d_][pi], B_qkT[d_][pi]
                        AT, BAT = ATb[d_][pi], B_AT[d_][pi]
                        smb, Bsm = accb[d_], B_acc[d_]
                        pO, BpO = accb[2 + d_], B_acc[2 + d_]
                        tr, Btr = tr_next()

                        def fnt(e):
                            for j in range(2):
                                e.transpose(out=tr[:, j * 128:(j + 1) * 128], in_=QDt[:, prl, j * 128:(j + 1) * 128], identity=identb[:])
                            for j in range(2):
                                ins = e.transpose(out=tr[:, 256 + j * 128:256 + (j + 1) * 128], in_=KIt[:, prl, j * 128:(j + 1) * 128],
                                                  identity=identb[:])
                            return ins
                        em.op("pe", fnt, [Bblk, B_cb], [Btr])
                        copy_op("act", qk[:], tr[:, 0:512], [Btr], [Bqk])
                        yield

                        def fna(e):
                            e.matmul(smb[:, 0:128], lhsT=qk[:, 256:384], rhs=qk[:, 0:128], start=True, stop=False)
                            return e.matmul(smb[:, 0:128], lhsT=qk[:, 384:512], rhs=qk[:, 128:256], start=False, stop=True)
                        em.op("pe", fna, [Bqk], [Bsm])
                        mk = cst[:, C_MF:C_MF + 128] if d_ == 0 else cst[:, C_MB:C_MB + 128]
                        em.op("dve", lambda e: e.tensor_tensor(out=AT[:], in0=smb[:, 0:128], in1=mk, op=ALU.mult),
                              [Bsm, B_cst], [BAT])
                        if kind == 1:
                            em.op("pe", lambda e: e.matmul(smb[:, 130:131], lhsT=AT[:], rhs=onesb[:, 0:1], start=True, stop=True),
                                  [BAT, B_cb], [Bsm])
                        yield
                        corder = (0, 1) if d_ == 0 else (1, 0)
                        dbase = (kind * 2 + d_) * 8 + h * 2
                        for cix, c in enumerate(corder):
                            cr = slice(c * 64, (c + 1) * 64)
                            chunk = pg * 2 + c

                            def fno(e):
                                for dc in range(2):
                                    ins = e.matmul(pO[cr, :512], lhsT=qk[:, dc * 128 + c * 64: dc * 128 + (c + 1) * 64],
                                                   rhs=Sb[d_][:, dc, :], start=(dc == 0), stop=False)
                                if cix == 1:
                                    ins = e.matmul(pO[:, :512], lhsT=AT[:], rhs=VVt[:, prl, :], start=False, stop=True)
                                return ins
                            em.op("pe", fno, [BAT, Bblk, Bqk] + B_Sb[d_], [BpO])
                            if kind == 1:
                                def fnd(e):
                                    for dc in range(2):
                                        ins = e.matmul(smb[cr, 131:132], lhsT=qk[:, dc * 128 + c * 64: dc * 128 + (c + 1) * 64],
                                                       rhs=nb[d_][:, dc:dc + 1], start=(dc == 0), stop=(dc == 1))
                                    return ins
                                em.op("pe", fnd, [Bqk, B_nb[d_]], [Bsm])
                            for dc in range(2):
                                sidx = 4 + (par["s"] % 2)
                                par["s"] += 1
                                pS, BpS = accb[sidx], B_acc[sidx]
                                em.op("pe", lambda e: e.matmul(pS[:, :512], lhsT=KEt[cr, prl, dc * 128:(dc + 1) * 128], rhs=VVt[cr, prl, :],
                                                               start=True, stop=True), [Bblk], [BpS])
                                em.op("dve", lambda e: e.scalar_tensor_tensor(out=S32[d_][:, dc, :], in0=S32[d_][:, dc, :],
                                                                              scalar=dec[:, dbase + dc, chunk:chunk + 1], in1=pS[:, :512],
                                                                              op0=ALU.mult, op1=ALU.add), [B_S[d_][dc], B_dec, BpS], [B_S[d_][dc]])
                                copy_op("act", Sb[d_][:, dc, :], S32[d_][:, dc, :], [B_S[d_][dc]], [B_Sb[d_][dc]])
                            if kind == 1:
                                def fnn(e):
                                    for dc in range(2):
                                        ins = e.matmul(smb[:, 128 + dc:128 + dc + 1], lhsT=KEt[cr, prl, dc * 128:(dc + 1) * 128], rhs=onesb[cr, 0:1],
                                                       start=True, stop=True)
                                    return ins
                                em.op("pe", fnn, [Bblk, B_cb], [Bsm])
                                for dc in range(2):
                                    em.op("dve", lambda e: e.scalar_tensor_tensor(out=n32[d_][:, dc:dc + 1], in0=n32[d_][:, dc:dc + 1],
                                                                                  scalar=dec[:, dbase + dc, chunk:chunk + 1],
                                                                                  in1=smb[:, 128 + dc:128 + dc + 1],
                                                                                  op0=ALU.mult, op1=ALU.add), [B_n[d_], B_dec, Bsm], [B_n[d_]])
                                em.op("dve", lambda e: e.tensor_copy(out=nb[d_][:], in_=n32[d_][:]), [B_n[d_]], [B_nb[d_]])
                            yield
                        if kind == 0:
                            if first_visit:
                                copy_op("act", oacc[:, pg, :], pO[:, :512], [BpO], [B_oacc[pg]])
                            else:
                                em.op("dve", lambda e: e.tensor_tensor(out=oacc[:, pg, :], in0=pO[:, :512], in1=oacc[:, pg, :], op=ALU.add),
                                      [BpO, B_oacc[pg]], [B_oacc[pg]])
                        else:
                            rd, Brd = rden[d_], B_rden[d_]
                            em.op("dve", lambda e: e.tensor_copy(out=rd[:, 0:1], in_=smb[:, 130:131]), [Bsm], [Brd])
                            em.op("dve", lambda e: e.tensor_tensor(out=rd[:, 0:1], in0=smb[:, 131:132], in1=rd[:, 0:1], op=ALU.add),
                                  [Bsm, Brd], [Brd])
                            em.op("dve", lambda e: e.tensor_scalar(out=rd[:, 1:2], in0=rd[:, 0:1], scalar1=-1.0, scalar2=1.0,
                                                                   op0=ALU.mult, op1=ALU.max), [Brd], [Brd])
                            em.op("dve", lambda e: e.tensor_tensor(out=rd[:, 1:2], in0=rd[:, 0:1], in1=rd[:, 1:2], op=ALU.max),
                                  [Brd], [Brd])
                            em.op("dve", lambda e: e.reciprocal(out=rd[:, 2:3], in_=rd[:, 1:2]), [Brd], [Brd])
                            if first_visit:
                                em.op("dve", lambda e: e.tensor_scalar(out=oacc[:, pg, :], in0=pO[:, :512], scalar1=rd[:, 2:3], scalar2=None,
                                                                       op0=ALU.mult), [BpO, Brd], [B_oacc[pg]])
                            else:
                                em.op("dve", lambda e: e.scalar_tensor_tensor(out=oacc[:, pg, :], in0=pO[:, :512], scalar=rd[:, 2:3],
                                                                              in1=oacc[:, pg, :], op0=ALU.mult, op1=ALU.add),
                                      [BpO, Brd, B_oacc[pg]], [B_oacc[pg]])
                for step in range(NPAIR):
                    for d_ in range(2):
                        pg = step if d_ == 0 else NPAIR - 1 - step
                        blk = pg // PB
                        bi = (step // PB) % 2
                        if step % PB == 0:
                            if step == 0:
                                load_block(d_, blk, bi)
                            nblk = blk + 1 if d_ == 0 else blk - 1
                            if 0 <= nblk < NBLK:
                                load_block(d_, nblk, 1 - bi)
                    gens = [pair_body(0, step, step < NPAIR // 2), pair_body(1, step, step < NPAIR // 2)]
                    alive = True
                    while alive:
                        alive = False
                        for g in gens:
                            try:
                                next(g)
                                alive = True
                            except StopIteration:
                                pass
                nwsrc = (W["gla_out_norm"] if kind == 0 else W["mlstm_out_norm"])[l][h * 512:(h + 1) * 512]
                em.op("sp", lambda e: e.dma_start(out=nwbc[:], in_=nwsrc.partition_broadcast(128)), [], [B_nw], dma=True)
                for pg in range(NPAIR):
                    t, Bt = epf_next()
                    em.op("act", lambda e: e.activation(out=t[:], in_=oacc[:, pg, :], func=AF.Square, accum_out=st4[:, 0:1]),
                          [B_oacc[pg]], [Bt, B_ss])
                    rstd_from_ss(512)
                    em.op("dve", lambda e: e.scalar_tensor_tensor(out=t[:], in0=oacc[:, pg, :], scalar=st4[:, 1:2], in1=nwbc[:],
                                                                  op0=ALU.mult, op1=ALU.mult), [B_oacc[pg], B_rstd, B_nw], [Bt])
                    gi = par["g"] % 2
                    par["g"] += 1
                    em.op("sp", lambda e: e.dma_start(out=gtile[gi][:], in_=gate[pg * 128:(pg + 1) * 128, v0:v0 + 512]),
                          [B_gate[pg][v0 // 512]], [B_gtile[gi]], dma=True)
                    mt, Bm = epb_next()
                    em.op("dve", lambda e: e.tensor_tensor(out=mt[:], in0=t[:], in1=gtile[gi][:], op=ALU.mult), [Bt, B_gtile[gi]], [Bm])
                    em.op("sp", lambda e: e.dma_start(out=merged[pg * 128:(pg + 1) * 128, v0:v0 + 512], in_=mt[:]),
                          [Bm], [B_mg[pg][v0 // 512]], dma=True)
        em.barrier()
        m2.close()
        alloc_core()

        for p in range(NPASS):
            row0 = p * T
            for sub in range(NSUB):
                r0 = row0 + sub * 128
                em.op("sp", lambda e: e.dma_start(out=C.xb[:], in_=merged[r0:r0 + 128, :]), B_mg[r0 // 128], [C.B_xb], dma=True)
                transpose_to_XT(C.xb, C.B_xb, KC, sub, False)
            epi = resid_epilogue(src, Bsrc, xs, B_xs, row0, 1.0)
            linear(C.XT, C.B_XT, KC, NSUB, [([(W["w_out"][l], cb * 512, 512, 0)], epi) for cb in range(8)])

    def xattn(l, src, Bsrc):
        xa = contextlib.ExitStack()

        def sbx(name, shape, dt):
            return xa.enter_context(nc.sbuf_tensor(name + "_%d" % l, list(shape), dt))
        kT = sbx("kT", (128, 4, MEMT), BF16)
        Vb = sbx("Vb", (128, 2, 512), BF16)
        B_kT, B_Vb = Buf("kT"), Buf("Vb")
        qT = sbx("qT", (128, 4, 128), BF16)
        B_qT = Buf("qT")
        pb = sbx("pb", (128, MEMT), BF16)
        B_pb = Buf("pb")
        pT = sbx("pT", (128, 2, 128), BF16)
        B_pT = Buf("pT")
        ob = sbx("ob", (128, 512), BF16)
        B_ob = Buf("ob")
        OT = sbx("OT", (128, 4, T), BF16)
        B_OT = [[Buf("OT%d_%d" % (c, s)) for s in range(NSUB)] for c in range(4)]
        sm4 = [sbx("sm%d" % i, (128, 4), F32) for i in range(4)]
        B_sm4 = [Buf("sm%d" % i) for i in range(4)]
        pb4 = [sbx("pb%d" % i, (128, MEMT), BF16) for i in range(4)]
        B_pb4 = [Buf("pb%d" % i) for i in range(4)]
        pT4 = [sbx("pT%d" % i, (128, 2, 128), BF16) for i in range(4)]
        B_pT4 = [Buf("pT%d" % i) for i in range(4)]
        scale = 128.0 ** -0.5
        load_wcol(W["mem_norm"][l])
        B_mem = [[Buf("mem", const=True)] for _ in range(2)]
        norm_stage(mem_in, B_mem, 0, 2)

        def epi_k(bi, sub, pst, Bps):
            t, Bt = epb_next()
            copy_op("act", t[:], pst[:, :512], [Bps], [Bt])
            tr, Btr = tr_next()

            def fn(e):
                for hh in range(4):
                    ins = e.transpose(out=tr[:, hh * 128:(hh + 1) * 128], in_=t[:, hh * 128:(hh + 1) * 128], identity=identb[:])
                return ins
            em.op("pe", fn, [Bt, B_cb], [Btr])
            copy_op("dve", kT[:, :, sub * 128:(sub + 1) * 128], tr[:, 0:512].rearrange("p (h t) -> p h t", t=128), [Btr], [B_kT])

        def epi_v(bi, sub, pst, Bps):
            copy_op("act", Vb[:, sub, :], pst[:, :512], [Bps], [B_Vb])
        linear(C.XT, C.B_XT, KC, 2, [([(W["xattn_wk"][l], 0, 512, 0)], epi_k), ([(W["xattn_wv"][l], 0, 512, 0)], epi_v)])
        load_wcol(W["xattn_norm"][l])
        for p in range(NPASS):
            row0 = p * T
            norm_stage(src, Bsrc, row0, NSUB)

            qsave = {}

            def epi_q(bi, sub, pst, Bps):
                t, Bt = epb_next()
                em.op("dve", lambda e: e.tensor_scalar(out=t[:], in0=pst[:, :512], scalar1=scale, scalar2=None, op0=ALU.mult), [Bps], [Bt])
                qsave[sub] = (t, Bt)

            def post_q(bi, sub):
                t, Bt = qsave[sub]
                tr, Btr = tr_next()

                def fn(e):
                    for hh in range(4):
                        ins = e.transpose(out=tr[:, hh * 128:(hh + 1) * 128], in_=t[:, hh * 128:(hh + 1) * 128], identity=identb[:])
                    return ins
                em.op("pe", fn, [Bt, B_cb], [Btr])
                copy_op("dve", qT[:], tr[:, 0:512].rearrange("p (h t) -> p h t", t=128), [Btr], [B_qT])

                def head_gen(hh):
                    pS, BpS = acc_next()
                    em.op("pe", lambda e: e.matmul(pS[:, :MEMT], lhsT=qT[:, hh, :], rhs=kT[:, hh, :], start=True, stop=True),
                          [B_qT, B_kT], [BpS])
                    yield
                    smh, Bsmh = sm4[hh], B_sm4[hh]
                    em.op("dve", lambda e: e.tensor_reduce(out=smh[:, 0:1], in_=pS[:, :MEMT], axis=mybir.AxisListType.X, op=ALU.max),
                          [BpS], [Bsmh])
                    em.op("dve", lambda e: e.tensor_scalar(out=smh[:, 1:2], in0=smh[:, 0:1], scalar1=-1.0, scalar2=None, op0=ALU.mult),
                          [Bsmh], [Bsmh])
                    em.op("act", lambda e: e.activation(out=pb4[hh][:], in_=pS[:, :MEMT], func=AF.Exp, bias=smh[:, 1:2], accum_out=smh[:, 2:3]),
                          [BpS, Bsmh], [B_pb4[hh], Bsmh])
                    em.op("dve", lambda e: e.reciprocal(out=smh[:, 3:4], in_=smh[:, 2:3]), [Bsmh], [Bsmh])
                    yield
                    tr2, Btr2 = tr_next()

                    def fn2(e):
                        for mt in range(2):
                            ins = e.transpose(out=tr2[:, mt * 128:(mt + 1) * 128], in_=pb4[hh][:, mt * 128:(mt + 1) * 128], identity=identb[:])
                        return ins
                    em.op("pe", fn2, [B_pb4[hh], B_cb], [Btr2])
                    copy_op("act", pT4[hh][:], tr2[:, 0:256].rearrange("p (m t) -> p m t", t=128), [Btr2], [B_pT4[hh]])
                    yield
                    pO, BpO = acc_next()

                    def fn3(e):
                        for mt in range(2):
                            ins = e.matmul(pO[:, 0:128], lhsT=pT4[hh][:, mt, :], rhs=Vb[:, mt, hh * 128:(hh + 1) * 128], start=(mt == 0), stop=(mt == 1))
                        return ins
                    em.op("pe", fn3, [B_pT4[hh], B_Vb], [BpO])
                    yield
                    em.op("dve", lambda e: e.tensor_scalar(out=ob[:, hh * 128:(hh + 1) * 128], in0=pO[:, 0:128], scalar1=smh[:, 3:4], scalar2=None,
                                                           op0=ALU.mult), [BpO, Bsmh], [B_ob])
                gens = [head_gen(hh) for hh in range(4)]
                alive = True
                while alive:
                    alive = False
                    for g in gens:
                        try:
                            next(g)
                            alive = True
                        except StopIteration:
                            pass
                transpose_to_XT(ob, B_ob, 4, sub, False, xt=OT, bxt=B_OT)
            linear(C.XT, C.B_XT, KC, NSUB, [([(W["xattn_wq"][l], 0, 512, 0)], epi_q, post_q)])
            epi = resid_epilogue(src, Bsrc, xs, B_xs, row0, 1.0)
            linear(OT, B_OT, 4, NSUB, [([(W["xattn_wo"][l], cb * 512, 512, 0)], epi) for cb in range(8)])
        em.barrier()
        xa.close()

    cur, Bcur = x_in, B_xin
    for l in range(DEPTH):
        if "ffn1" in stages:
            ffn(l, "ffn1", cur, Bcur)
            if KDBG in (0, 3):
                cur, Bcur = xs, B_xs
        if "mix" in stages:
            mixer(l, cur, Bcur)
            cur, Bcur = xs, B_xs
        if "xattn" in stages:
            xattn(l, cur, Bcur)
            cur, Bcur = xs, B_xs
        if "ffn2" in stages:
            ffn(l, "ffn2", cur, Bcur)
            cur, Bcur = xs, B_xs
    fin = final_stage(cur, Bcur)
    em.finish()
    fin.close()
    free_core()
    es.close()
    return nc


_WNAMES = ("ffn1_norm", "ffn1_w_gate", "ffn1_w_up", "ffn1_w_down", "mix_norm", "w_in", "gla_w_lr", "gla_b_lr",
           "gla_out_norm", "mlstm_gate_b", "mlstm_out_norm", "w_out", "xattn_norm", "mem_norm", "xattn_wq",
           "xattn_wk", "xattn_wv", "xattn_wo", "ffn2_norm", "ffn2_w_gate", "ffn2_w_up", "ffn2_w_down")


def run(seqs, mems, weights, L, DFF, DEPTH, stages=("ffn1", "mix", "xattn", "ffn2"), n_cores=None):
    nc = build(L, DFF, DEPTH, stages)
    consts = make_consts()
    n_cores = n_cores or len(seqs)
    shared = {k: np.ascontiguousarray(v, dtype=np.float32) for k, v in weights.items()}
    shared["mlstm_gate_b"] = shared["mlstm_gate_b"].reshape(DEPTH, 16)
    in_maps = []
    for c in range(n_cores):
        m = dict(shared)
        m["x"] = np.ascontiguousarray(seqs[c % len(seqs)], dtype=np.float32)
        m["mem"] = np.ascontiguousarray(mems[c % len(mems)], dtype=np.float32)
        m["consts"] = consts
        in_maps.append(m)
    res = run_bass_kernel_spmd(nc, in_maps, core_ids=list(range(n_cores)))
    return [r["y"] for r in res.results]


def kernel(x_prompt, x_sample, mem_prompt, mem_sample, **w):
    seqs = [x_prompt[i] for i in range(x_prompt.shape[0])] + [x_sample[i] for i in range(x_sample.shape[0])]
    mems = [mem_prompt[i] for i in range(mem_prompt.shape[0])] + [mem_sample[i] for i in range(mem_sample.shape[0])]
    weights = {k: w[k] for k in _WNAMES}
    weights["final_norm"] = w["final_norm"]
    L = x_prompt.shape[1]
    ys = run(seqs, mems, weights, L, w["ffn1_w_gate"].shape[2], w["ffn1_w_gate"].shape[0], n_cores=8)
    nb = x_prompt.shape[0]
    y_prompt = np.stack(ys[:nb]).astype(np.float32)
    y_sample = np.stack(ys[nb:nb + x_sample.shape[0]]).astype(np.float32)
    return (y_prompt, y_sample)
```
